# Optimizing a Trainium2 kernel written in Bass

```python
import math
import jax
import jax.numpy as jnp
from jax import lax
import numpy as np

D_MODEL = 1024
BATCH = 2
SEQ = 8192
DEPTH = 4
DEC_BATCH = 4
DEC_SEQ = 8192
PAST_LEN = 128

HEAD_DIM = 64
DA_HEADS = 4
DA_VDIM = 2 * HEAD_DIM
CONV_CH = 512
CONV_WIDTH = 31
WA_HEADS = 8
WA_KV_HEADS = 2
WA_GROUP = WA_HEADS // WA_KV_HEADS
WINDOW = 128
BLOCK = 128
Q_BLOCK = 128
MEM_TOKENS = 256
MA_HEADS = 4
MA_HEAD_DIM = 128
BRANCH_WIDTH = 512
N_BRANCHES = 4
D_FF = 4 * D_MODEL
REL_BUCKETS = 32
REL_MAX_DIST = 128
REL_HEADS = 2 * DA_HEADS + WA_HEADS
EPS = 1e-6
NEG_INF = -1e30
SPLIT_SIZES = (DA_HEADS * 2 * HEAD_DIM, DA_HEADS * 2 * HEAD_DIM, DA_HEADS * DA_VDIM, 2 * CONV_CH, WA_HEADS * HEAD_DIM, WA_KV_HEADS * HEAD_DIM, WA_KV_HEADS * HEAD_DIM, MA_HEADS * MA_HEAD_DIM, N_BRANCHES * D_MODEL)
IN_WIDTH = sum(SPLIT_SIZES)

kernel_name = 'hybrid_gated_bidir_encoder'


def rms_norm(x, g):
    xf = x.astype(jnp.float32)
    y = xf * lax.rsqrt(jnp.mean(xf * xf, axis=-1, keepdims=True) + EPS)
    return (y * g.astype(jnp.float32)).astype(x.dtype)


def layer_norm(x, g, b):
    xf = x.astype(jnp.float32)
    mu = jnp.mean(xf, axis=-1, keepdims=True)
    xc = xf - mu
    var = jnp.mean(xc * xc, axis=-1, keepdims=True)
    return (xc * lax.rsqrt(var + EPS) * g.astype(jnp.float32) + b.astype(jnp.float32)).astype(x.dtype)


def rel_bucket(rel):
    nb = REL_BUCKETS // 2
    max_exact = nb // 2
    ret = jnp.where(rel > 0, nb, 0)
    n = jnp.abs(rel)
    nf = jnp.maximum(n, 1).astype(jnp.float32)
    large = max_exact + (jnp.log(nf / max_exact) / math.log(REL_MAX_DIST / max_exact) * (nb - max_exact)).astype(jnp.int32)
    large = jnp.minimum(large, nb - 1)
    return ret + jnp.where(n < max_exact, n, large)


def diff_attention(q, k, v, qk_g, lam_params, lam_init, subln_g, rel_table):
    B, S = q.shape[0], q.shape[1]
    q = rms_norm(q, qk_g[0])
    k = rms_norm(k, qk_g[1])
    lp = lam_params.astype(jnp.float32)
    lam = jnp.exp(jnp.sum(lp[0] * lp[1])) - jnp.exp(jnp.sum(lp[2] * lp[3])) + lam_init
    nq = S // Q_BLOCK
    qb = q.reshape(B, nq, Q_BLOCK, DA_HEADS, 2, HEAD_DIM).transpose(1, 0, 2, 3, 4, 5)
    kpos = jnp.arange(S)
    scale = HEAD_DIM ** -0.5

    def one_block(args):
        qi, i = args
        s = jnp.einsum('bqhmd,bkhmd->bhmqk', qi, k, preferred_element_type=jnp.float32) * scale
        qpos = i * Q_BLOCK + jnp.arange(Q_BLOCK)
        bias = rel_table[rel_bucket(kpos[None, :] - qpos[:, None])]
        bias = bias.reshape(Q_BLOCK, S, DA_HEADS, 2).transpose(2, 3, 0, 1).astype(jnp.float32)
        p = jax.nn.softmax(s + bias, axis=-1)
        pd = p[:, :, 0] - lam * p[:, :, 1]
        return jnp.einsum('bhqk,bkhe->bqhe', pd.astype(v.dtype), v)

    o = lax.map(one_block, (qb, jnp.arange(nq)))
    o = o.transpose(1, 0, 2, 3, 4).reshape(B, S, DA_HEADS, DA_VDIM)
    o = rms_norm(o, subln_g) * (1.0 - lam_init)
    return o.reshape(B, S, DA_HEADS * DA_VDIM)


def conv_module(u, conv_w, conv_b, ln_g, ln_b):
    a, g = jnp.split(u, 2, axis=-1)
    z = a * jax.nn.sigmoid(g)
    pad = CONV_WIDTH // 2
    z = lax.conv_general_dilated(z, conv_w[:, None, :].astype(z.dtype), window_strides=(1,), padding=[(pad, pad)], dimension_numbers=('NWC', 'WIO', 'NWC'), feature_group_count=CONV_CH) + conv_b
    z = layer_norm(z, ln_g, ln_b)
    return jax.nn.silu(z)


def window_attention(q, k, v, qk_g, sink, rel_table):
    B, S = q.shape[0], q.shape[1]
    q = rms_norm(q, qk_g[0])
    k = rms_norm(k, qk_g[1])
    n = S // BLOCK
    qb = q.reshape(B, n, BLOCK, WA_KV_HEADS, WA_GROUP, HEAD_DIM)

    def band(t):
        t = t.reshape(B, n, BLOCK, WA_KV_HEADS, HEAD_DIM)
        tp = jnp.pad(t, ((0, 0), (1, 1), (0, 0), (0, 0), (0, 0)))
        return jnp.concatenate([tp[:, :-2], tp[:, 1:-1], tp[:, 2:]], axis=2)

    kb = band(k)
    vb = band(v)
    s = jnp.einsum('bnqhgd,bnkhd->bnhgqk', qb, kb, preferred_element_type=jnp.float32) * (HEAD_DIM ** -0.5)
    qoff = jnp.arange(BLOCK)
    koff = jnp.arange(3 * BLOCK) - BLOCK
    rel = koff[None, :] - qoff[:, None]
    bias = rel_table[rel_bucket(rel)].transpose(2, 0, 1).reshape(WA_KV_HEADS, WA_GROUP, BLOCK, 3 * BLOCK)
    kpos = jnp.arange(n)[:, None] * BLOCK + koff[None, :]
    valid = (jnp.abs(rel) <= WINDOW)[None] & ((kpos >= 0) & (kpos < S))[:, None, :]
    s = jnp.where(valid[None, :, None, None], s + bias.astype(jnp.float32), NEG_INF)
    sink_l = sink.astype(jnp.float32).reshape(WA_KV_HEADS, WA_GROUP)[None, None, :, :, None, None]
    m = jnp.maximum(jnp.max(s, axis=-1, keepdims=True), sink_l)
    e = jnp.exp(s - m)
    p = e / (jnp.sum(e, axis=-1, keepdims=True) + jnp.exp(sink_l - m))
    o = jnp.einsum('bnhgqk,bnkhd->bnqhgd', p.astype(v.dtype), vb)
    return o.reshape(B, S, WA_HEADS * HEAD_DIM)


def memory_attention(q, mem_n, w_mem_kv, qk_g):
    B, S = q.shape[0], q.shape[1]
    M = mem_n.shape[1]
    kv = mem_n @ w_mem_kv
    k, v = jnp.split(kv, 2, axis=-1)
    k = k.reshape(B, M, MA_HEADS, MA_HEAD_DIM)
    v = v.reshape(B, M, MA_HEADS, MA_HEAD_DIM)
    q = rms_norm(q, qk_g[0])
    k = rms_norm(k, qk_g[1])
    s = jnp.einsum('bshd,bmhd->bhsm', q, k, preferred_element_type=jnp.float32) * (MA_HEAD_DIM ** -0.5)
    p = jax.nn.softmax(s, axis=-1)
    o = jnp.einsum('bhsm,bmhd->bshd', p.astype(v.dtype), v)
    return o.reshape(B, S, MA_HEADS * MA_HEAD_DIM)


def encoder_layer(x, mem, layer_idx, rel_bias, norm1_g, w_in, da_qk_g, da_lambda, da_subln_g, conv_w, conv_b, conv_ln_g, conv_ln_b, wa_qk_g, wa_sink, mem_norm_g, w_mem_kv, ma_qk_g, w_branch, w_out, norm2_g, w_ff1, w_ff2):
    B, S, _ = x.shape
    h = rms_norm(x, norm1_g)
    u = h @ w_in
    split_idx = tuple(int(i) for i in np.cumsum(SPLIT_SIZES)[:-1])
    da_q, da_k, da_v, conv_in, wa_q, wa_k, wa_v, ma_q, gate_logits = jnp.split(u, split_idx, axis=-1)
    lam_init = 0.8 - 0.6 * math.exp(-0.3 * layer_idx)
    br_da = diff_attention(da_q.reshape(B, S, DA_HEADS, 2, HEAD_DIM), da_k.reshape(B, S, DA_HEADS, 2, HEAD_DIM), da_v.reshape(B, S, DA_HEADS, DA_VDIM), da_qk_g, da_lambda, lam_init, da_subln_g, rel_bias[:, :2 * DA_HEADS])
    br_conv = conv_module(conv_in, conv_w, conv_b, conv_ln_g, conv_ln_b)
    br_wa = window_attention(wa_q.reshape(B, S, WA_HEADS, HEAD_DIM), wa_k.reshape(B, S, WA_KV_HEADS, HEAD_DIM), wa_v.reshape(B, S, WA_KV_HEADS, HEAD_DIM), wa_qk_g, wa_sink, rel_bias[:, 2 * DA_HEADS:])
    br_ma = memory_attention(ma_q.reshape(B, S, MA_HEADS, MA_HEAD_DIM), rms_norm(mem, mem_norm_g), w_mem_kv, ma_qk_g)
    gates = jax.nn.sigmoid(gate_logits).reshape(B, S, N_BRANCHES, D_MODEL)
    merged = gates[:, :, 0] * (br_da @ w_branch[0])
    merged = merged + gates[:, :, 1] * (br_conv @ w_branch[1])
    merged = merged + gates[:, :, 2] * (br_wa @ w_branch[2])
    merged = merged + gates[:, :, 3] * (br_ma @ w_branch[3])
    x = x + merged @ w_out
    h2 = rms_norm(x, norm2_g)
    f = jnp.square(jax.nn.relu(h2 @ w_ff1)) @ w_ff2
    return x + f


def run_trunk(x, mem, rel_bias, norm1_g, w_in, da_qk_g, da_lambda, da_subln_g, conv_w, conv_b, conv_ln_g, conv_ln_b, wa_qk_g, wa_sink, mem_norm_g, w_mem_kv, ma_qk_g, w_branch, w_out, norm2_g, w_ff1, w_ff2):
    for l in range(DEPTH):
        x = encoder_layer(x, mem, l, rel_bias, norm1_g[l], w_in[l], da_qk_g[l], da_lambda[l], da_subln_g[l], conv_w[l], conv_b[l], conv_ln_g[l], conv_ln_b[l], wa_qk_g[l], wa_sink[l], mem_norm_g[l], w_mem_kv[l], ma_qk_g[l], w_branch[l], w_out[l], norm2_g[l], w_ff1[l], w_ff2[l])
    return x


def setup_inputs(seed: int = 0) -> dict:
    key = jax.random.key(seed)
    ks = jax.random.split(key, 24)

    def nrm(k, shape, scale):
        return jax.random.normal(k, shape, jnp.float32) * scale

    def gain(k, shape):
        return 1.0 + 0.02 * jax.random.normal(k, shape, jnp.float32)

    return {
        'x_prompt': nrm(ks[0], (BATCH, SEQ, D_MODEL), 1.0),
        'x_sample': nrm(ks[1], (DEC_BATCH, DEC_SEQ, D_MODEL), 1.0),
        'mem_prompt': nrm(ks[2], (BATCH, MEM_TOKENS, D_MODEL), 1.0),
        'mem_sample': nrm(ks[3], (DEC_BATCH, MEM_TOKENS, D_MODEL), 1.0),
        'rel_bias': nrm(ks[4], (REL_BUCKETS, REL_HEADS), 0.5),
        'norm1_g': gain(ks[5], (DEPTH, D_MODEL)),
        'w_in': nrm(ks[6], (DEPTH, D_MODEL, IN_WIDTH), D_MODEL ** -0.5),
        'da_qk_g': gain(ks[7], (DEPTH, 2, HEAD_DIM)),
        'da_lambda': nrm(ks[8], (DEPTH, 4, HEAD_DIM), 0.1),
        'da_subln_g': gain(ks[9], (DEPTH, DA_VDIM)),
        'conv_w': nrm(ks[10], (DEPTH, CONV_WIDTH, CONV_CH), CONV_WIDTH ** -0.5),
        'conv_b': nrm(ks[11], (DEPTH, CONV_CH), 0.02),
        'conv_ln_g': gain(ks[12], (DEPTH, CONV_CH)),
        'conv_ln_b': nrm(ks[13], (DEPTH, CONV_CH), 0.02),
        'wa_qk_g': gain(ks[14], (DEPTH, 2, HEAD_DIM)),
        'wa_sink': nrm(ks[15], (DEPTH, WA_HEADS), 0.5),
        'mem_norm_g': gain(ks[16], (DEPTH, D_MODEL)),
        'w_mem_kv': nrm(ks[17], (DEPTH, D_MODEL, 2 * MA_HEADS * MA_HEAD_DIM), D_MODEL ** -0.5),
        'ma_qk_g': gain(ks[18], (DEPTH, 2, MA_HEAD_DIM)),
        'w_branch': nrm(ks[19], (DEPTH, N_BRANCHES, BRANCH_WIDTH, D_MODEL), BRANCH_WIDTH ** -0.5),
        'w_out': nrm(ks[20], (DEPTH, D_MODEL, D_MODEL), D_MODEL ** -0.5),
        'norm2_g': gain(ks[21], (DEPTH, D_MODEL)),
        'w_ff1': nrm(ks[22], (DEPTH, D_MODEL, D_FF), D_MODEL ** -0.5),
        'w_ff2': nrm(ks[23], (DEPTH, D_FF, D_MODEL), D_FF ** -0.5),
    }


def reference(x_prompt, x_sample, mem_prompt, mem_sample, rel_bias, norm1_g, w_in, da_qk_g, da_lambda, da_subln_g, conv_w, conv_b, conv_ln_g, conv_ln_b, wa_qk_g, wa_sink, mem_norm_g, w_mem_kv, ma_qk_g, w_branch, w_out, norm2_g, w_ff1, w_ff2):
    y_prompt = run_trunk(x_prompt, mem_prompt, rel_bias, norm1_g, w_in, da_qk_g, da_lambda, da_subln_g, conv_w, conv_b, conv_ln_g, conv_ln_b, wa_qk_g, wa_sink, mem_norm_g, w_mem_kv, ma_qk_g, w_branch, w_out, norm2_g, w_ff1, w_ff2)
    y_sample = run_trunk(x_sample, mem_sample, rel_bias, norm1_g, w_in, da_qk_g, da_lambda, da_subln_g, conv_w, conv_b, conv_ln_g, conv_ln_b, wa_qk_g, wa_sink, mem_norm_g, w_mem_kv, ma_qk_g, w_branch, w_out, norm2_g, w_ff1, w_ff2)
    return (y_prompt, y_sample)
```

```python
import math
from contextlib import ExitStack

import numpy as np
import concourse.bass as bass
import concourse.mybir as mybir
from concourse.bass_utils import run_bass_kernel_spmd

F32 = mybir.dt.float32
BF16 = mybir.dt.bfloat16
AF = mybir.ActivationFunctionType
ALU = mybir.AluOpType

D = 1024
DEPTH = 4
EPS = 1e-6
IN_W = 7936
C_DAQ, C_DAK, C_DAV, C_CA, C_CG, C_WAQ, C_WAK, C_WAV, C_MAQ, C_GATE = 0, 512, 1024, 1536, 2048, 2560, 3072, 3200, 3328, 3840
WG = 1408
LG = 1535
LW = 1279
WGW = 1152
SP_N1, SP_N2, SP_MN, SP_DAQK, SP_WAQK, SP_MAQK, SP_SUB, SP_CVB, SP_LNG, SP_LNB, SP_SINK, SP_CW, SP_LAM = 0, 8, 16, 24, 26, 28, 30, 31, 35, 39, 43, 51, 175
NSP = 175 + 256
SAME_ENGINE_SYNC = True


class Buf:
    __slots__ = ("name", "w", "r", "sem")

    def __init__(self, name):
        self.name = name
        self.w = None
        self.r = {}
        self.sem = None


class Sched:
    ENG = ("pe", "act", "dve", "pool", "sp")

    def __init__(self, nc, es, n_dma_sems=88):
        self.nc = nc
        self.csem = {e: es.enter_context(nc.semaphore("c_" + e)) for e in ("pe", "act", "dve", "pool")}
        self.dsems = [es.enter_context(nc.semaphore("d%d" % i)) for i in range(n_dma_sems)]
        self.semcnt = [0] * n_dma_sems
        self.free = list(range(n_dma_sems))
        self.idx = {e: 0 for e in self.ENG}
        self.base = {e: 0 for e in self.ENG}
        self.waited = {e: {} for e in self.ENG}
        self.ops = {e: [] for e in self.ENG}
        self.dirty = set()
        self.phase_first = {e: 0 for e in self.ENG}
        self.n_instr = 0

    def op(self, eng, fn, r=(), w=(), dma=None):
        deps = []
        for b in r:
            if b.w is not None:
                deps.append(b.w)
        for b in w:
            if b.w is not None:
                deps.append(b.w)
            for k, v in b.r.items():
                deps.append((k[0], k[1], v))
        waits = {}
        wd = self.waited[eng]
        for kind, key, val in deps:
            if kind == "c" and key == eng and (eng == "pe" or not SAME_ENGINE_SYNC):
                continue
            k = (kind, key)
            if wd.get(k, -1) >= val:
                continue
            if waits.get(k, -1) < val:
                waits[k] = val
        for k, v in waits.items():
            wd[k] = v
        if dma is None:
            ev = ("c", eng, self.idx[eng])
            self.idx[eng] += 1
        else:
            if dma.sem is None:
                dma.sem = self.free.pop()
            self.semcnt[dma.sem] += 16
            ev = ("d", dma.sem, self.semcnt[dma.sem])
            self.dirty.add(dma.sem)
        self.ops[eng].append((waits, fn, ev))
        for b in r:
            k = (ev[0], ev[1])
            if b.r.get(k, -1) < ev[2]:
                b.r[k] = ev[2]
        for b in w:
            b.w = ev
            b.r = {}
        return ev

    def flush(self):
        nc = self.nc
        for e in self.ENG:
            waits = {}
            for E in ("pe", "act", "dve", "pool"):
                if self.idx[E] > self.phase_first[E]:
                    last = self.idx[E] - 1
                    if self.waited[e].get(("c", E), -1) < last:
                        waits[("c", E)] = last
                        self.waited[e][("c", E)] = last
            for sidx in sorted(self.dirty):
                v = self.semcnt[sidx]
                if self.waited[e].get(("d", sidx), -1) < v:
                    waits[("d", sidx)] = v
                    self.waited[e][("d", sidx)] = v
            self.ops[e].append((waits, None, None))
        sigs = {e: set() for e in self.ENG}
        for e in self.ENG:
            for waits, fn, ev in self.ops[e]:
                for (kind, key), val in waits.items():
                    if kind == "c":
                        sigs[key].add(val)
        rank = {}
        for e in self.ENG:
            rank[e] = {ix: self.base[e] + 1 + i for i, ix in enumerate(sorted(sigs[e]))}
        csem, dsems = self.csem, self.dsems
        engobj = {"pe": nc.tensor, "act": nc.scalar, "dve": nc.vector, "pool": nc.gpsimd, "sp": nc.sync}

        def make(e):
            ops = self.ops[e]
            sg = sigs[e]

            def body(eng):
                for waits, fn, ev in ops:
                    for (kind, key), val in waits.items():
                        if kind == "c":
                            eng.wait_ge(csem[key], rank[key][val])
                        else:
                            eng.wait_ge(dsems[key], val)
                    if fn is None:
                        continue
                    ins = fn(eng)
                    self.n_instr += 1
                    if ev[0] == "c":
                        if ev[2] in sg:
                            ins.then_inc(csem[e], 1)
                    else:
                        ins.then_inc(dsems[ev[1]], 16)
            return body

        with nc.Block() as block:
            block.tensor(make("pe"))
            block.scalar(make("act"))
            block.vector(make("dve"))
            block.gpsimd(make("pool"))
            block.sync(make("sp"))
        for e in self.ENG:
            self.base[e] += len(sigs[e])
            self.ops[e] = []
            self.phase_first[e] = self.idx[e]
        self.dirty = set()
        self.free = list(range(len(self.dsems)))


PH_LIMIT = [10 ** 9]
P0STOP = [0]
_PH = [0]


def _phase_done():
    _PH[0] += 1
    return _PH[0] >= PH_LIMIT[0]


def build_program(S, L, debug=False):
    _PH[0] = 0
    NT = S // 512
    NB = S // 128
    nc = bass.Bass("TRN2", target_bir_lowering=False)
    okind = "ExternalOutput" if debug else "Internal"

    def din(name, shape, dt=F32):
        return nc.dram_tensor(name, list(shape), dt, kind="ExternalInput")

    def dscr(name, shape, dt=BF16, kind=None):
        return nc.dram_tensor(name, list(shape), dt, kind=kind or okind)

    xT = din("xT", [D, S]).ap()
    memT = din("memT", [D, 256]).ap()
    oh = din("oh", [33, LG + LW]).ap()
    tabx = din("tabx", [33, 16]).ap()
    ident_d = din("ident", [128, 128]).ap()
    spd = din("sp", [L, 128, NSP]).ap()
    w_in = din("w_in", [L, D, IN_W]).ap()
    w_kv = din("w_mem_kv", [L, D, 1024]).ap()
    w_br = din("w_branch", [L, 2048, D]).ap()
    w_out = din("w_out", [L, D, D]).ap()
    w_ff1 = din("w_ff1", [L, D, 4096]).ap()
    w_ff2 = din("w_ff2", [L, 4096, D]).ap()
    yT = nc.dram_tensor("yT", [D, S], F32, kind="ExternalOutput").ap()

    GVR_h = dscr("GVR", [16 * 128 * LG], F32)
    GVR = GVR_h.ap()
    QDA = dscr("QDA", [512, S]).ap()
    KDA = dscr("KDA", [512, S]).ap()
    VDA = dscr("VDA", [S, 512]).ap()
    Z = dscr("Z", [512, S + 30]).ap()
    QWA = dscr("QWA", [512, S]).ap()
    KWA = dscr("KWA", [128, S]).ap()
    VWA = dscr("VWA", [S, 128]).ap()
    QMA = dscr("QMA", [512, S]).ap()
    GATE = dscr("GATE", [4096, S]).ap()
    BR = dscr("BR", [1536, S]).ap()
    XM = dscr("XM", [D, S], F32).ap()
    XA = dscr("XA", [D, S], F32).ap()

    es = ExitStack()
    with es:
        sch = Sched(nc, es)
        op = sch.op

        uniq = [0]

        def sb(st, name, shape, dt):
            uniq[0] += 1
            return st.enter_context(nc.sbuf_tensor("%s_%d" % (name, uniq[0]), list(shape), dt))

        PS = es.enter_context(nc.psum_tensor("PS", [128, 8, 512], F32))
        ones = sb(es, "ones", [128, 128], BF16)
        bones = sb(es, "bones", [128, 128], BF16)
        ident = sb(es, "ident_s", [128, 128], F32)
        spt = sb(es, "spt", [128, L, NSP], F32)
        gsc = sb(es, "gsc", [128, L, 40], F32)
        Km = sb(es, "Km", [128, 4, 256], BF16)
        Vm = sb(es, "Vm", [128, 2, 512], BF16)
        Bones, Bident, Bspt, Bgsc = Buf("ones"), Buf("ident"), Buf("spt"), Buf("gsc")

        def psb():
            return [Buf("ps%d" % i) for i in range(8)]

        def dma(eng, out, in_, r=(), w=(), sembuf=None):
            op(eng, lambda e, o=out, i=in_: e.dma_start(out=o, in_=i), r=r, w=w, dma=sembuf)

        with ExitStack() as st:
            tab = sb(st, "tab", [33, 16], F32)
            tabb = sb(st, "tabb", [33, 16, 128], F32)
            ohs = sb(st, "ohs", [33, LG + LW], F32)
            gvs = sb(st, "gvs", [128, LG], F32)
            zt = sb(st, "zt", [128, 4, 16], BF16)
            lamt = sb(st, "lamt", [128, 8], F32)
            lamp = sb(st, "lamp", [128, 256], F32)
            Btab, Btabb, Bohs, Bgvs, Bzt, Blam, Blamp = [Buf(n) for n in ("tab", "tabb", "ohs", "gvs", "zt", "lam", "lamp")]
            pb = psb()
            op("dve", lambda e: e.memset(ones[:], 1.0), w=[Bones])
            op("dve", lambda e: e.memset(bones[:], 0.0), w=[Bones])
            op("dve", lambda e: e.memset(bones[0:64, 0:64], 1.0), w=[Bones])
            op("dve", lambda e: e.memset(bones[64:128, 64:128], 1.0), w=[Bones])
            op("dve", lambda e: e.memset(zt[:], 0.0), w=[Bzt])
            dma("sp", ident[:], ident_d, w=[Bident], sembuf=Bident)
            dma("sp", spt[:], spd.rearrange("l p n -> p l n"), w=[Bspt], sembuf=Bspt)
            dma("sp", tab[:], tabx, w=[Btab], sembuf=Btab)
            dma("sp", ohs[:], oh, w=[Bohs], sembuf=Bohs)
            Zv = Z.rearrange("(c p) s -> p c s", p=128)
            dma("pool", Zv[:, :, 0:15], zt[:, :, 0:15], r=[Bzt], sembuf=Bzt)
            dma("pool", Zv[:, :, S + 15:S + 30], zt[:, :, 0:15], r=[Bzt], sembuf=Bzt)
            if P0STOP[0] == 1:
                sch.flush()
                return nc
            for col in range(16):
                op("dve", lambda e, col=col: e.tensor_copy(out=tabb[:, col, :], in_=tab[:, col:col + 1].to_broadcast([33, 128])),
                   r=[Btab], w=[Btabb])
            for col in range(16):
                base, ln = (0, LG) if col < 8 else (LG, LW)
                n0 = 0
                while n0 < ln:
                    n = min(512, ln - n0)
                    bk = (col * 4 + n0 // 512) % 8
                    op("pe", lambda e, col=col, bk=bk, n0=n0, n=n, base=base: e.matmul(
                        PS[:, bk, 0:n], tabb[:, col, :], ohs[:, base + n0:base + n0 + n], start=True, stop=True),
                       r=[Btabb, Bohs], w=[pb[bk]])
                    op("act", lambda e, bk=bk, n0=n0, n=n: e.activation(out=gvs[:, n0:n0 + n], in_=PS[:, bk, 0:n], func=AF.Identity),
                       r=[pb[bk]], w=[Bgvs])
                    n0 += n
                dst = bass.AP(tensor=GVR_h, offset=col * 128 * LG, ap=[[LG, 128], [1, ln]])
                dma("pool", dst, gvs[:, 0:ln], r=[Bgvs], sembuf=Bgvs)
            if P0STOP[0] == 2:
                sch.flush()
                return nc
            for l in range(L):
                lam_init = 0.8 - 0.6 * math.exp(-0.3 * l)

                def mulc(dst0, src0, n, c, l=l):
                    op("dve", lambda e: e.tensor_scalar(out=gsc[:, l, dst0:dst0 + n], in0=spt[:, l, src0:src0 + n],
                                                        scalar1=float(c), scalar2=None, op0=ALU.mult),
                       r=[Bspt], w=[Bgsc])
                mulc(0, SP_N1, 8, 32.0)
                mulc(8, SP_N2, 8, 32.0)
                mulc(16, SP_MN, 8, 32.0)
                mulc(24, SP_DAQK, 1, 1.0)
                mulc(25, SP_DAQK + 1, 1, 8.0)
                mulc(26, SP_WAQK, 1, 1.0)
                mulc(27, SP_WAQK + 1, 1, 8.0)
                mulc(28, SP_MAQK, 1, 1.0)
                mulc(29, SP_MAQK + 1, 1, math.sqrt(128.0))
                mulc(30, SP_SUB, 1, math.sqrt(128.0) * (1.0 - lam_init))
                if P0STOP[0] == 3:
                    continue
                op("dve", lambda e, l=l: e.tensor_tensor(out=lamp[:, 0:64], in0=spt[:, l, SP_LAM:SP_LAM + 64],
                                                         in1=spt[:, l, SP_LAM + 64:SP_LAM + 128], op=ALU.mult), r=[Bspt], w=[Blamp])
                op("dve", lambda e, l=l: e.tensor_tensor(out=lamp[:, 64:128], in0=spt[:, l, SP_LAM + 128:SP_LAM + 192],
                                                         in1=spt[:, l, SP_LAM + 192:SP_LAM + 256], op=ALU.mult), r=[Bspt], w=[Blamp])
                op("dve", lambda e: e.reduce_sum(out=lamt[:, 0:1], in_=lamp[:, 0:64], axis=mybir.AxisListType.X), r=[Blamp], w=[Blam])
                op("dve", lambda e: e.reduce_sum(out=lamt[:, 1:2], in_=lamp[:, 64:128], axis=mybir.AxisListType.X), r=[Blamp], w=[Blam])
                if P0STOP[0] == 4:
                    continue
                op("act", lambda e: e.activation(out=lamt[:, 2:4], in_=lamt[:, 0:2], func=AF.Exp), r=[Blam], w=[Blam])
                if P0STOP[0] == 5:
                    continue
                op("dve", lambda e: e.tensor_tensor(out=lamt[:, 5:6], in0=lamt[:, 3:4], in1=lamt[:, 2:3], op=ALU.subtract),
                   r=[Blam], w=[Blam])
                op("dve", lambda e, li=lam_init: e.tensor_scalar(out=lamt[:, 4:5], in0=lamt[:, 5:6], scalar1=-float(li), scalar2=None,
                                                                 op0=ALU.add), r=[Blam], w=[Blam])
                op("dve", lambda e, l=l: e.tensor_copy(out=gsc[:, l, 31:32], in_=lamt[:, 4:5]), r=[Blam], w=[Bgsc])
                op("act", lambda e, l=l: e.activation(out=gsc[:, l, 32:40], in_=spt[:, l, SP_SINK:SP_SINK + 8], func=AF.Exp),
                   r=[Bspt], w=[Bgsc])
            sch.flush()
            if _phase_done():
                return nc

        def rstd_op(out_ap, ps_ap, c, r, w):
            op("act", lambda e: e.activation(out=out_ap, in_=ps_ap, func=AF.Sqrt, bias=float(c), scale=1.0), r=r, w=w)
            op("dve", lambda e: e.reciprocal(out=out_ap, in_=out_ap), r=[], w=w)

        def wload(dst_ap, src_ap, buf):
            dma("pool", dst_ap, src_ap, w=[buf], sembuf=buf)

        for l in range(L):
            xin = xT if l == 0 else XA
            xout = yT if l == L - 1 else XA
            xin_v = xin.rearrange("(c p) s -> p c s", p=128)
            xout_v = xout.rearrange("(c p) s -> p c s", p=128)
            XM_v = XM.rearrange("(c p) s -> p c s", p=128)

            with ExitStack() as st:
                W = sb(st, "W1", [128, 8, 8192], BF16)
                xt1 = sb(st, "xt1", [128, 8, 512], F32)
                hTs = [sb(st, "hT%d" % i, [128, 8, 512], BF16) for i in range(2)]
                sq = [sb(st, "sq%d" % i, [128, 512], BF16) for i in range(2)]
                rs = [sb(st, "rs%d" % i, [128, 512], F32) for i in range(2)]
                sg = [sb(st, "sg%d" % i, [128, 512], F32) for i in range(2)]
                stg = [sb(st, "stg%d" % i, [128, 4, 512], BF16) for i in range(4)]
                Bxt1 = Buf("xt1")
                BhTs = [Buf("hT0"), Buf("hT1")]
                Bsq = [Buf("sq0"), Buf("sq1")]
                Brs = [Buf("rs0"), Buf("rs1")]
                Bsg = [Buf("sg0"), Buf("sg1")]
                Bstg = [Buf("stg%d" % i) for i in range(4)]
                pb = psb()
                NWG = 8
                BW = [Buf("W%d" % i) for i in range(NWG + 1)]
                wv = w_in[l].rearrange("(c p) f -> p c f", p=128)
                for g in range(NWG):
                    f0 = g * 992
                    wload(W[:, :, f0:f0 + 992], wv[:, :, f0:f0 + 992], BW[g])
                for j in range(2):
                    for d2 in range(2):
                        o = 7936 + j * 128 + d2 * 64
                        wload(W[:, :, o:o + 64], wv[:, :, C_WAK + 64 * j:C_WAK + 64 * j + 64], BW[NWG])

                def wbufs(f0, n):
                    if f0 >= 7936:
                        return [BW[NWG]]
                    return [BW[g] for g in range(f0 // 992, (f0 + n - 1) // 992 + 1)]

                cnt = {"ps": 0, "ss": 0, "sq": 0, "rs": 0, "stg": 0, "sg": 0}

                def nxt(k, n):
                    v = cnt[k] % n
                    cnt[k] += 1
                    return v

                def norm1(t):
                    t0n = t * 512
                    hb_ = t % 2
                    dma("sp", xt1[:], xin_v[:, :, t0n:t0n + 512], w=[Bxt1], sembuf=Bxt1)
                    ssb = 6 + nxt("ss", 2)
                    for c in range(8):
                        q = nxt("sq", 2)
                        op("act", lambda e, q=q, c=c: e.activation(out=sq[q][:], in_=xt1[:, c, :], func=AF.Square),
                           r=[Bxt1], w=[Bsq[q]])
                        op("pe", lambda e, q=q, c=c, ssb=ssb: e.matmul(PS[:, ssb, :], ones[:], sq[q][:], start=(c == 0), stop=(c == 7)),
                           r=[Bsq[q]], w=[pb[ssb]])
                    rq = nxt("rs", 2)
                    rstd_op(rs[rq][:], PS[:, ssb, :], EPS * 1024, [pb[ssb]], [Brs[rq]])
                    for c in range(8):
                        op("dve", lambda e, c=c, rq=rq, hb_=hb_: e.scalar_tensor_tensor(
                            out=hTs[hb_][:, c, :], in0=xt1[:, c, :], scalar=gsc[:, l, c:c + 1], in1=rs[rq][:], op0=ALU.mult, op1=ALU.mult),
                           r=[Bxt1, Brs[rq]], w=[BhTs[hb_]])

                norm1(0)
                for t in range(NT):
                    t0 = t * 512
                    hT = hTs[t % 2]
                    BhT = BhTs[t % 2]
                    pending = []
                    pcount = [0]

                    def run_pending(upto=None):
                        while pending and (upto is None or pending[0][0] <= upto):
                            pending.pop(0)[1]()

                    def proj(f0, nout=128):
                        bk = nxt("ps", 6)
                        for c in range(8):
                            op("pe", lambda e, c=c, bk=bk, hT=hT: e.matmul(PS[0:nout, bk, :], W[:, c, f0:f0 + nout], hT[:, c, :],
                                                                      start=(c == 0), stop=(c == 7)),
                               r=[BhT] + wbufs(f0, nout), w=[pb[bk]])
                        pcount[0] += 1
                        run_pending(pcount[0] - 2)
                        return bk

                    def qknorm(bk, hd, gcol, dst_ap, dstbuf):
                        q = nxt("sq", 2)
                        op("act", lambda e: e.activation(out=sq[q][:], in_=PS[:, bk, :], func=AF.Square), r=[pb[bk]], w=[Bsq[q]])
                        ssb = 6 + nxt("ss", 2)
                        lhs = bones if hd == 64 else ones
                        rq = nxt("rs", 2)

                        def stage_b():
                            op("pe", lambda e: e.matmul(PS[:, ssb, :], lhs[:], sq[q][:], start=True, stop=True), r=[Bsq[q]], w=[pb[ssb]])
                            rstd_op(rs[rq][:], PS[:, ssb, :], EPS * hd, [pb[ssb]], [Brs[rq]])
                            op("dve", lambda e: e.scalar_tensor_tensor(out=dst_ap, in0=PS[:, bk, :], scalar=gsc[:, l, gcol:gcol + 1],
                                                                       in1=rs[rq][:], op0=ALU.mult, op1=ALU.mult),
                               r=[pb[bk], Brs[rq]], w=[dstbuf])
                        pending.append((pcount[0], stage_b))

                    def store(dst_ap, src_ap, sbuf_):
                        pending.append((pcount[0], lambda: dma("pool", dst_ap, src_ap, r=[sbuf_], sembuf=sbuf_)))

                    def fm_view(T_, r0, nchunk):
                        return T_[r0:r0 + 128 * nchunk, :].rearrange("(c p) s -> p c s", p=128)

                    for (f0, gcol, DST) in ((C_DAQ, 24, QDA), (C_DAK, 25, KDA)):
                        s_ = nxt("stg", 4)
                        for c4 in range(4):
                            bk = proj(f0 + 128 * c4)
                            qknorm(bk, 64, gcol, stg[s_][:, c4, :], Bstg[s_])
                        store(fm_view(DST, 0, 4)[:, :, t0:t0 + 512], stg[s_][:], Bstg[s_])
                    run_pending()
                    s_ = nxt("stg", 4)
                    for sbk in range(4):
                        bk = nxt("ps", 6)
                        for c in range(8):
                            op("pe", lambda e, c=c, bk=bk, sbk=sbk, hT=hT: e.matmul(PS[:, bk, :], hT[:, c, 128 * sbk:128 * sbk + 128],
                                                                               W[:, c, C_DAV:C_DAV + 512], start=(c == 0), stop=(c == 7)),
                               r=[BhT] + wbufs(C_DAV, 512), w=[pb[bk]])
                        op("act", lambda e, bk=bk, sbk=sbk, s_=s_: e.activation(out=stg[s_][:, sbk, :], in_=PS[:, bk, :], func=AF.Identity),
                           r=[pb[bk]], w=[Bstg[s_]])
                    store(VDA[t0:t0 + 512, :].rearrange("(sb p) f -> p sb f", p=128), stg[s_][:], Bstg[s_])
                    s_ = nxt("stg", 4)
                    for c4 in range(4):
                        bkg = proj(C_CG + 128 * c4)
                        g_ = nxt("sg", 2)
                        op("act", lambda e, bkg=bkg, g_=g_: e.activation(out=sg[g_][:], in_=PS[:, bkg, :], func=AF.Sigmoid),
                           r=[pb[bkg]], w=[Bsg[g_]])
                        bka = proj(C_CA + 128 * c4)
                        op("dve", lambda e, bka=bka, g_=g_, c4=c4, s_=s_: e.tensor_tensor(out=stg[s_][:, c4, :], in0=PS[:, bka, :],
                                                                                         in1=sg[g_][:], op=ALU.mult),
                           r=[pb[bka], Bsg[g_]], w=[Bstg[s_]])
                    store(fm_view(Z, 0, 4)[:, :, 15 + t0:15 + t0 + 512], stg[s_][:], Bstg[s_])
                    def qknorm64(bk, gcol, dst_ap, dstbuf):
                        q = nxt("sq", 2)
                        op("act", lambda e: e.activation(out=sq[q][0:64, :], in_=PS[0:64, bk, :], func=AF.Square), r=[pb[bk]], w=[Bsq[q]])
                        ssb = 6 + nxt("ss", 2)
                        rq = nxt("rs", 2)

                        def stage_b():
                            op("pe", lambda e: e.matmul(PS[0:64, ssb, :], ones[0:64, 0:64], sq[q][0:64, :], start=True, stop=True),
                               r=[Bsq[q]], w=[pb[ssb]])
                            rstd_op(rs[rq][0:64, :], PS[0:64, ssb, :], EPS * 64, [pb[ssb]], [Brs[rq]])
                            op("dve", lambda e: e.scalar_tensor_tensor(out=dst_ap, in0=PS[0:64, bk, :], scalar=gsc[0:64, l, gcol:gcol + 1],
                                                                       in1=rs[rq][0:64, :], op0=ALU.mult, op1=ALU.mult),
                               r=[pb[bk], Brs[rq]], w=[dstbuf])
                        pending.append((pcount[0], stage_b))

                    QWv = QWA.rearrange("(c p) s -> p c s", p=64)
                    for half in range(2):
                        s_ = nxt("stg", 4)
                        for c4 in range(4):
                            bk = proj(C_WAQ + 64 * (4 * half + c4), nout=64)
                            qknorm64(bk, 26, stg[s_][0:64, c4, :], Bstg[s_])
                        store(QWv[:, 4 * half:4 * half + 4, t0:t0 + 512], stg[s_][0:64, :, :], Bstg[s_])
                    KWv = KWA.rearrange("(c p) s -> p c s", p=64)
                    s_ = nxt("stg", 4)
                    for j in range(2):
                        bk = proj(C_WAK + 64 * j, nout=64)
                        qknorm64(bk, 27, stg[s_][0:64, j, :], Bstg[s_])
                    store(KWv[:, :, t0:t0 + 512], stg[s_][0:64, 0:2, :], Bstg[s_])
                    run_pending()
                    s_ = nxt("stg", 4)
                    for sbk in range(4):
                        bk = nxt("ps", 6)
                        for c in range(8):
                            op("pe", lambda e, c=c, bk=bk, sbk=sbk, hT=hT: e.matmul(PS[:, bk, 0:128], hT[:, c, 128 * sbk:128 * sbk + 128],
                                                                               W[:, c, C_WAV:C_WAV + 128], start=(c == 0), stop=(c == 7)),
                               r=[BhT] + wbufs(C_WAV, 128), w=[pb[bk]])
                        op("act", lambda e, bk=bk, sbk=sbk, s_=s_: e.activation(out=stg[s_][:, sbk, 0:128], in_=PS[:, bk, 0:128], func=AF.Identity),
                           r=[pb[bk]], w=[Bstg[s_]])
                    store(VWA[t0:t0 + 512, :].rearrange("(sb p) f -> p sb f", p=128), stg[s_][:, :, 0:128], Bstg[s_])
                    s_ = nxt("stg", 4)
                    for c4 in range(4):
                        bk = proj(C_MAQ + 128 * c4)
                        qknorm(bk, 128, 28, stg[s_][:, c4, :], Bstg[s_])
                    store(fm_view(QMA, 0, 4)[:, :, t0:t0 + 512], stg[s_][:], Bstg[s_])
                    if t + 1 < NT:
                        run_pending()
                        norm1(t + 1)
                    for g8 in range(8):
                        s_ = nxt("stg", 4)
                        for c4 in range(4):
                            bk = proj(C_GATE + 512 * g8 + 128 * c4)
                            op("act", lambda e, bk=bk, c4=c4, s_=s_: e.activation(out=stg[s_][:, c4, :], in_=PS[:, bk, :], func=AF.Sigmoid),
                               r=[pb[bk]], w=[Bstg[s_]])
                        store(fm_view(GATE, 512 * g8, 4)[:, :, t0:t0 + 512], stg[s_][:], Bstg[s_])
                    run_pending()
                sch.flush()
                if _phase_done():
                    return nc

            with ExitStack() as st:
                KT = [sb(st, "KT%d" % i, [128, S], BF16) for i in range(2)]
                VV = [sb(st, "VV%d" % i, [128, NB, 128], BF16) for i in range(2)]
                GG = [[sb(st, "GG%d_%d" % (i, m), [128, WG], F32) for m in range(2)] for i in range(2)]
                QT = [[sb(st, "QT%d_%d" % (i, m), [128, 512], BF16) for m in range(2)] for i in range(3)]
                PT = [sb(st, "PT%d" % i, [128, 2, 512], BF16) for i in range(4)]
                TM = [sb(st, "TM%d" % i, [128, 2, 512], F32) for i in range(2)]
                ACC = [[sb(st, "ACC%d_%d" % (m, k), [128, 2, 512], F32) for k in range(3)] for m in range(2)]
                onesf = sb(st, "onesf", [128, 128], F32)
                r_ = [sb(st, "r%d" % i, [128, 512], F32) for i in range(2)]
                a_ = [sb(st, "a%d" % i, [128, 512], F32) for i in range(2)]
                oo = sb(st, "oo", [128, 512], F32)
                sq2 = sb(st, "sq2", [128, 512], BF16)
                rs2 = sb(st, "rs2", [128, 512], F32)
                ost = [sb(st, "ost%d" % i, [128, 512], BF16) for i in range(2)]
                BKT, BVV = [Buf("KT0"), Buf("KT1")], [Buf("VV0"), Buf("VV1")]
                BGG = [[Buf("G"), Buf("G")], [Buf("G"), Buf("G")]]
                BQT = [Buf("QT") for _ in range(3)]
                BPT = [Buf("PT") for _ in range(4)]
                BTM = [Buf("TM") for _ in range(2)]
                BACC = [[Buf("ACC") for _ in range(3)] for _ in range(2)]
                Bonesf = Buf("onesf")
                Br, Ba = [Buf("r0"), Buf("r1")], [Buf("a0"), Buf("a1")]
                Boo, Bsq2, Brs2 = Buf("oo"), Buf("sq2"), Buf("rs2")
                Bost = [Buf("ost0"), Buf("ost1")]
                pb = psb()
                VDAv = VDA.rearrange("(kb p) f -> p kb f", p=128)
                op("dve", lambda e: e.memset(onesf[:], 1.0), w=[Bonesf])
                for i in range(3):
                    for m in range(2):
                        op("dve", lambda e, i=i, m=m: e.memset(QT[i][m][:], 0.0), w=[BQT[i]])
                tasks = []
                cnt2 = {"q": 0, "p": 0, "tm": 0, "s": 0, "o": 0}

                def mk_task(h, qc, m, kp, qi):
                    hb = h % 2
                    t0 = qc * 512
                    pr = slice(64 * m, 64 * m + 64)
                    G, BG = GG[hb][m], BGG[hb][m]
                    ob, db = 4 + 2 * m, 5 + 2 * m
                    ssb = 7
                    sbk = 2 * (cnt2["s"] % 2)
                    cnt2["s"] += 1
                    pi = cnt2["p"] % 4
                    cnt2["p"] += 1
                    d0 = 2 * kp - 4 * qc
                    near = -2 <= d0 <= 4
                    ti = None
                    if near:
                        ti = cnt2["tm"] % 2
                        cnt2["tm"] += 1
                    acc, Bacc = ACC[m][0], BACC[m][0]

                    def pre():
                        if qc == 0 and m == 0 and kp == 0:
                            dma("sp", KT[hb][:], KDA[128 * h:128 * h + 128, :], w=[BKT[hb]], sembuf=BKT[hb])
                            dma("sp", VV[hb][:], VDAv[:, :, 128 * h:128 * h + 128], w=[BVV[hb]], sembuf=BVV[hb])
                            for m2 in range(2):
                                col = 2 * h + m2
                                src = bass.AP(tensor=GVR_h, offset=col * 128 * LG + 127, ap=[[LG - 1, 128], [1, WG]])
                                dma("sp", GG[hb][m2][:], src, w=[BGG[hb][m2]], sembuf=BGG[hb][m2])
                        if m == 0 and kp == 0:
                            for m2 in range(2):
                                dma("sp", QT[qi][m2][64 * m2:64 * m2 + 64, :], QDA[128 * h + 64 * m2:128 * h + 64 * m2 + 64, t0:t0 + 512],
                                    w=[BQT[qi]], sembuf=BQT[qi])

                    def qk():
                        for i2 in range(2):
                            kb = 2 * kp + i2
                            op("pe", lambda e, kb=kb, i2=i2: e.matmul(
                                PS[:, sbk + i2, :], KT[hb][:, 128 * kb:128 * kb + 128], QT[qi][m][:, :], start=True, stop=True),
                               r=[BKT[hb], BQT[qi]], w=[pb[sbk + i2]])

                    def ex():
                        if near:
                            for i2 in range(2):
                                d = d0 + i2
                                c0 = 640 - 128 * d
                                op("dve", lambda e, i2=i2, c0=c0: e.tensor_tensor(
                                    out=TM[ti][:, i2, :], in0=PS[:, sbk + i2, :], in1=G[:, c0:c0 + 512], op=ALU.add),
                                   r=[pb[sbk + i2], BG], w=[BTM[ti]])
                            op("act", lambda e: e.activation(out=PT[pi][:], in_=TM[ti][:], func=AF.Exp),
                               r=[BTM[ti]], w=[BPT[pi]])
                        else:
                            bc = WG - 1 if d0 < 0 else 0
                            op("act", lambda e: e.activation(
                                out=PT[pi][:], in_=PS[:, sbk:sbk + 2, :], func=AF.Exp, bias=G[:, bc:bc + 1], scale=1.0),
                               r=[pb[sbk], pb[sbk + 1], BG], w=[BPT[pi]])

                    def pv():
                        for i2 in range(2):
                            kb = 2 * kp + i2
                            first, last = (kb == 0), (kb == NB - 1)
                            op("pe", lambda e, kb=kb, i2=i2, first=first, last=last: e.matmul(
                                PS[:, ob, :], VV[hb][:, kb, :], PT[pi][:, i2, :], start=first, stop=last),
                               r=[BVV[hb], BPT[pi]], w=[pb[ob]])
                        if kp % 2 == 0:
                            if kp == 0:
                                op("dve", lambda e: e.tensor_copy(out=acc[:], in_=PT[pi][:]), r=[BPT[pi]], w=[Bacc])
                            else:
                                op("dve", lambda e: e.tensor_tensor(out=acc[:], in0=acc[:], in1=PT[pi][:], op=ALU.add),
                                   r=[BPT[pi]], w=[Bacc])
                        else:
                            for i2 in range(2):
                                op("pe", lambda e, i2=i2: e.matmul(PS[:, db, :], ones[:], PT[pi][:, i2, :],
                                                                    start=(kp == 1 and i2 == 0), stop=False),
                                   r=[BPT[pi]], w=[pb[db]])

                    def post():
                        if kp != NB // 2 - 1:
                            return
                        for i2 in range(2):
                            op("pe", lambda e, i2=i2: e.matmul(
                                PS[:, db, :], onesf[:], ACC[m][0][:, i2, :], start=False, stop=(i2 == 1)),
                               r=[BACC[m][0], Bonesf], w=[pb[db]])
                        op("dve", lambda e: e.reciprocal(out=r_[m][:], in_=PS[:, db, :]), r=[pb[db]], w=[Br[m]])
                        op("dve", lambda e: e.tensor_tensor(out=a_[m][:], in0=PS[:, ob, :], in1=r_[m][:], op=ALU.mult),
                           r=[pb[ob], Br[m]], w=[Ba[m]])
                        if m == 0:
                            return
                        op("dve", lambda e: e.scalar_tensor_tensor(out=oo[:], in0=a_[1][:], scalar=gsc[:, l, 31:32], in1=a_[0][:],
                                                                   op0=ALU.mult, op1=ALU.add), r=[Ba[0], Ba[1]], w=[Boo])
                        op("dve", lambda e: e.tensor_tensor(out=sq2[:], in0=oo[:], in1=oo[:], op=ALU.mult), r=[Boo], w=[Bsq2])
                        op("pe", lambda e: e.matmul(PS[:, ssb, :], ones[:], sq2[:], start=True, stop=True), r=[Bsq2], w=[pb[ssb]])
                        op("act", lambda e: e.activation(out=rs2[:], in_=PS[:, ssb, :], func=AF.Ln, bias=float(EPS * 128), scale=1.0),
                           r=[pb[ssb]], w=[Brs2])
                        op("act", lambda e: e.activation(out=rs2[:], in_=rs2[:], func=AF.Exp, scale=-0.5), r=[], w=[Brs2])
                        oi = cnt2["o"] % 2
                        cnt2["o"] += 1
                        op("dve", lambda e: e.scalar_tensor_tensor(out=ost[oi][:], in0=oo[:], scalar=gsc[:, l, 30:31], in1=rs2[:],
                                                                   op0=ALU.mult, op1=ALU.mult), r=[Boo, Brs2], w=[Bost[oi]])
                        dma("pool", BR[128 * h:128 * h + 128, t0:t0 + 512], ost[oi][:], r=[Bost[oi]], sembuf=Bost[oi])

                    return (pre, qk, ex, pv, post)

                for h in range(4):
                    for qc in range(NT):
                        qi = cnt2["q"] % 3
                        cnt2["q"] += 1
                        for m in range(2):
                            for kp in range(NB // 2):
                                tasks.append(mk_task(h, qc, m, kp, qi))
                NTK = len(tasks)
                for i in range(NTK + 4):
                    if i < NTK:
                        tasks[i][0]()
                        tasks[i][1]()
                        tasks[i][2]()
                    if 0 <= i - 2 < NTK:
                        tasks[i - 2][3]()
                    if 0 <= i - 4 < NTK:
                        tasks[i - 4][4]()
                sch.flush()
                if _phase_done():
                    return nc

            with ExitStack() as st:
                dg = sb(st, "dg", [128, 4, 31, 128], BF16)
                zt_ = [sb(st, "z%d" % i, [128, 4, 544], BF16) for i in range(2)]
                yy = sb(st, "yy", [128, 4, 512], F32)
                y2 = [sb(st, "y2_%d" % i, [128, 512], BF16) for i in range(2)]
                yb = [sb(st, "yb_%d" % i, [128, 512], BF16) for i in range(2)]
                mm_ = sb(st, "mm", [128, 512], F32)
                msq = sb(st, "msq", [128, 512], F32)
                var = sb(st, "var", [128, 512], F32)
                rsd = sb(st, "rsd", [128, 512], F32)
                t1 = [sb(st, "t1_%d" % i, [128, 512], F32) for i in range(2)]
                sgm = [sb(st, "sgm_%d" % i, [128, 512], F32) for i in range(2)]
                Bsgm = [Buf("sgm"), Buf("sgm")]
                cst = [sb(st, "cst%d" % i, [128, 4, 512], BF16) for i in range(2)]
                Bdg, Byy = Buf("dg"), Buf("yy")
                Bz = [Buf("z0"), Buf("z1")]
                By2, Byb = [Buf("y2"), Buf("y2")], [Buf("yb"), Buf("yb")]
                Bmm, Bmsq, Bvar, Brsd = Buf("mm"), Buf("msq"), Buf("var"), Buf("rsd")
                Bt1 = [Buf("t1"), Buf("t1")]
                Bcst = [Buf("cst"), Buf("cst")]
                pb = psb()
                for cc in range(4):
                    for j in range(31):
                        op("dve", lambda e, cc=cc, j=j: e.tensor_scalar(out=dg[:, cc, j, :], in0=ident[:],
                                                                        scalar1=spt[:, l, SP_CW + cc * 31 + j:SP_CW + cc * 31 + j + 1],
                                                                        scalar2=None, op0=ALU.mult), w=[Bdg])
                Zv = Z.rearrange("(c p) s -> p c s", p=128)
                BRv = BR[512:1024, :].rearrange("(c p) s -> p c s", p=128)
                c2 = ct = 0
                for t in range(NT):
                    t0 = t * 512
                    zb = t % 2
                    dma("sp", zt_[zb][:, :, 0:542], Zv[:, :, t0:t0 + 542], w=[Bz[zb]], sembuf=Bz[zb])
                    for cc in range(4):
                        bk = cc % 4
                        for j in range(31):
                            op("pe", lambda e, cc=cc, j=j, bk=bk, zb=zb: e.matmul(PS[:, bk, :], dg[:, cc, j, :], zt_[zb][:, cc, j:j + 512],
                                                                                    start=(j == 0), stop=(j == 30)),
                               r=[Bdg, Bz[zb]], w=[pb[bk]])
                        op("act", lambda e, cc=cc, bk=bk: e.activation(out=yy[:, cc, :], in_=PS[:, bk, :], func=AF.Identity,
                                                                        bias=spt[:, l, SP_CVB + cc:SP_CVB + cc + 1], scale=1.0),
                           r=[pb[bk]], w=[Byy])
                        i2 = c2 % 2
                        c2 += 1
                        op("act", lambda e, cc=cc, i2=i2: e.activation(out=y2[i2][:], in_=yy[:, cc, :], func=AF.Square), r=[Byy], w=[By2[i2]])
                        op("dve", lambda e, cc=cc, i2=i2: e.tensor_copy(out=yb[i2][:], in_=yy[:, cc, :]), r=[Byy], w=[Byb[i2]])
                        op("pe", lambda e, cc=cc, i2=i2: e.matmul(PS[:, 4, :], ones[:], yb[i2][:], start=(cc == 0), stop=(cc == 3)),
                           r=[Byb[i2]], w=[pb[4]])
                        op("pe", lambda e, cc=cc, i2=i2: e.matmul(PS[:, 5, :], ones[:], y2[i2][:], start=(cc == 0), stop=(cc == 3)),
                           r=[By2[i2]], w=[pb[5]])
                    op("dve", lambda e: e.tensor_scalar(out=mm_[:], in0=PS[:, 4, :], scalar1=1.0 / 512, scalar2=None, op0=ALU.mult),
                       r=[pb[4]], w=[Bmm])
                    op("dve", lambda e: e.tensor_tensor(out=msq[:], in0=mm_[:], in1=mm_[:], op=ALU.mult), r=[Bmm], w=[Bmsq])
                    op("dve", lambda e: e.tensor_scalar(out=var[:], in0=PS[:, 5, :], scalar1=1.0 / 512, scalar2=None, op0=ALU.mult),
                       r=[pb[5]], w=[Bvar])
                    op("dve", lambda e: e.tensor_tensor(out=var[:], in0=var[:], in1=msq[:], op=ALU.subtract), r=[Bmsq], w=[Bvar])
                    rstd_op(rsd[:], var[:], EPS, [Bvar], [Brsd])
                    ci = t % 2
                    for cc in range(4):
                        ti = ct % 2
                        ct += 1
                        op("dve", lambda e, cc=cc, ti=ti: e.tensor_tensor(out=t1[ti][:], in0=yy[:, cc, :], in1=mm_[:], op=ALU.subtract),
                           r=[Byy, Bmm], w=[Bt1[ti]])
                        op("dve", lambda e, ti=ti: e.tensor_tensor(out=t1[ti][:], in0=t1[ti][:], in1=rsd[:], op=ALU.mult),
                           r=[Brsd], w=[Bt1[ti]])
                        op("dve", lambda e, cc=cc, ti=ti: e.tensor_scalar(out=t1[ti][:], in0=t1[ti][:],
                                                                           scalar1=spt[:, l, SP_LNG + cc:SP_LNG + cc + 1],
                                                                           scalar2=spt[:, l, SP_LNB + cc:SP_LNB + cc + 1],
                                                                           op0=ALU.mult, op1=ALU.add), r=[], w=[Bt1[ti]])
                        op("act", lambda e, ti=ti: e.activation(out=sgm[ti][:], in_=t1[ti][:], func=AF.Sigmoid), r=[Bt1[ti]], w=[Bsgm[ti]])
                        op("dve", lambda e, cc=cc, ti=ti, ci=ci: e.tensor_tensor(out=cst[ci][:, cc, :], in0=t1[ti][:], in1=sgm[ti][:], op=ALU.mult),
                           r=[Bt1[ti], Bsgm[ti]], w=[Bcst[ci]])
                    dma("pool", BRv[:, :, t0:t0 + 512], cst[ci][:], r=[Bcst[ci]], sembuf=Bcst[ci])
                sch.flush()
                if _phase_done():
                    return nc

            with ExitStack() as st:
                KW = [sb(st, "KW%d" % i, [128, S], BF16) for i in range(2)]
                VW = [sb(st, "VW%d" % i, [128, NB, 128], BF16) for i in range(2)]
                QW = [sb(st, "QW%d" % i, [128, S], BF16) for i in range(2)]
                GW = [sb(st, "GW%d" % i, [128, WGW], F32) for i in range(2)]
                TW = [sb(st, "TW%d" % i, [128, 512], F32) for i in range(3)]
                PW = [sb(st, "PW%d" % i, [128, 512], BF16) for i in range(4)]
                rw = [sb(st, "rw%d" % i, [64, 512], F32) for i in range(2)]
                wst = [sb(st, "wst%d" % i, [64, 512], BF16) for i in range(2)]
                BKW, BVW, BQW = [Buf("KW"), Buf("KW")], [Buf("VW"), Buf("VW")], [Buf("QW"), Buf("QW")]
                BGW = [Buf("GW"), Buf("GW")]
                BTW = [Buf("TW") for _ in range(3)]
                BPW = [Buf("PW") for _ in range(4)]
                Brw = [Buf("rw"), Buf("rw")]
                Bwst = [Buf("wst"), Buf("wst")]
                pb = psb()
                VWAv = VWA.rearrange("(kb p) f -> p kb f", p=128)
                for i in range(2):
                    op("dve", lambda e, i=i: e.memset(KW[i][:], 0.0), w=[BKW[i]])
                    op("dve", lambda e, i=i: e.memset(QW[i][:], 0.0), w=[BQW[i]])
                    op("dve", lambda e, i=i: e.memset(VW[i][:], 0.0), w=[BVW[i]])
                cn4 = {"tw": 0, "pw": 0, "s": 0, "fin": 0}
                tasks4 = []

                def mk4(hh, qc, ki, kbs, par):
                    j = hh // 4
                    hb = hh % 2
                    t0 = qc * 512
                    kb = kbs[ki]
                    ob, db = 4 + 2 * par, 5 + 2 * par
                    d = kb - 4 * qc
                    c0 = 512 - 128 * d
                    sbk = cn4["s"] % 4
                    cn4["s"] += 1
                    ti = cn4["tw"] % 3
                    cn4["tw"] += 1
                    pi = cn4["pw"] % 4
                    cn4["pw"] += 1
                    first, last = (ki == 0), (ki == len(kbs) - 1)

                    def pre():
                        if qc == 0 and ki == 0:
                            if hh % 4 == 0:
                                dma("sp", KW[j][0:64, :], KWA[64 * j:64 * j + 64, :], w=[BKW[j]], sembuf=BKW[j])
                                dma("sp", VW[j][:, :, 0:64], VWAv[:, :, 64 * j:64 * j + 64], w=[BVW[j]], sembuf=BVW[j])
                            dma("sp", QW[hb][0:64, :], QWA[64 * hh:64 * hh + 64, :], w=[BQW[hb]], sembuf=BQW[hb])
                            src = bass.AP(tensor=GVR_h, offset=(8 + hh) * 128 * LG + 127, ap=[[LG - 1, 128], [1, WGW]])
                            dma("sp", GW[hb][:], src, w=[BGW[hb]], sembuf=BGW[hb])

                    def qk():
                        op("pe", lambda e: e.matmul(
                            PS[:, sbk, :], KW[j][:, 128 * kb:128 * kb + 128], QW[hb][:, t0:t0 + 512], start=True, stop=True),
                           r=[BKW[j], BQW[hb]], w=[pb[sbk]])

                    def ex():
                        op("dve", lambda e: e.tensor_tensor(
                            out=TW[ti][:], in0=PS[:, sbk, :], in1=GW[hb][:, c0:c0 + 512], op=ALU.add),
                           r=[pb[sbk], BGW[hb]], w=[BTW[ti]])
                        op("act", lambda e: e.activation(out=PW[pi][:], in_=TW[ti][:], func=AF.Exp),
                           r=[BTW[ti]], w=[BPW[pi]])

                    def pv():
                        op("pe", lambda e: e.matmul(
                            PS[:, ob, :], VW[j][:, kb, :], PW[pi][:], start=first, stop=last),
                           r=[BVW[j], BPW[pi]], w=[pb[ob]])
                        op("pe", lambda e: e.matmul(
                            PS[:, db, :], ones[:], PW[pi][:], start=first, stop=last),
                           r=[BPW[pi]], w=[pb[db]])

                    def post():
                        if not last:
                            return
                        op("dve", lambda e: e.tensor_scalar(
                            out=rw[par][:], in0=PS[0:64, db, :], scalar1=gsc[0:64, l, 32 + hh:33 + hh], scalar2=None, op0=ALU.add),
                           r=[pb[db]], w=[Brw[par]])
                        op("dve", lambda e: e.reciprocal(out=rw[par][:], in_=rw[par][:]), r=[], w=[Brw[par]])
                        op("dve", lambda e: e.tensor_tensor(out=wst[par][:], in0=PS[0:64, ob, :], in1=rw[par][:], op=ALU.mult),
                           r=[pb[ob], Brw[par]], w=[Bwst[par]])
                        dma("pool", BR[1024 + 64 * hh:1024 + 64 * hh + 64, t0:t0 + 512], wst[par][:], r=[Bwst[par]], sembuf=Bwst[par])

                    return (pre, qk, ex, pv, post)

                for hh in range(8):
                    for qc in range(NT):
                        par = cn4["fin"] % 2
                        cn4["fin"] += 1
                        kbs = [kb for kb in range(4 * qc - 1, 4 * qc + 5) if 0 <= kb < NB]
                        for ki in range(len(kbs)):
                            tasks4.append(mk4(hh, qc, ki, kbs, par))
                NT4 = len(tasks4)
                for i in range(NT4 + 3):
                    if i < NT4:
                        tasks4[i][0]()
                        tasks4[i][1]()
                        tasks4[i][2]()
                    if 0 <= i - 3 < NT4:
                        tasks4[i - 3][3]()
                        tasks4[i - 3][4]()
                sch.flush()
                if _phase_done():
                    return nc

            with ExitStack() as st:
                Wkv = sb(st, "Wkv", [128, 8, 1024], BF16)
                mt = sb(st, "mt", [128, 8, 256], F32)
                mn = sb(st, "mn", [128, 8, 256], BF16)
                sq = [sb(st, "sq%d" % i, [128, 512], BF16) for i in range(2)]
                rs = sb(st, "rs", [128, 512], F32)
                BWkv, Bmt, Bmn = Buf("Wkv"), Buf("mt"), Buf("mn")
                BKm, BVm = Buf("Km"), Buf("Vm")
                Bsq = [Buf("sq"), Buf("sq")]
                Brs = Buf("rs")
                pb = psb()
                cps = csq = 0
                wload(Wkv[:], w_kv[l].rearrange("(c p) f -> p c f", p=128), BWkv)
                dma("sp", mt[:], memT.rearrange("(c p) s -> p c s", p=128), w=[Bmt], sembuf=Bmt)
                for c in range(8):
                    q = csq % 2
                    csq += 1
                    op("act", lambda e, q=q, c=c: e.activation(out=sq[q][:, 0:256], in_=mt[:, c, :], func=AF.Square), r=[Bmt], w=[Bsq[q]])
                    op("pe", lambda e, q=q, c=c: e.matmul(PS[:, 4, 0:256], ones[:], sq[q][:, 0:256], start=(c == 0), stop=(c == 7)),
                       r=[Bsq[q]], w=[pb[4]])
                rstd_op(rs[:, 0:256], PS[:, 4, 0:256], EPS * 1024, [pb[4]], [Brs])
                for c in range(8):
                    op("dve", lambda e, c=c: e.scalar_tensor_tensor(out=mn[:, c, :], in0=mt[:, c, :], scalar=gsc[:, l, 16 + c:17 + c],
                                                                    in1=rs[:, 0:256], op0=ALU.mult, op1=ALU.mult), r=[Bmt, Brs], w=[Bmn])
                for h in range(4):
                    bk = h % 4
                    for c in range(8):
                        op("pe", lambda e, c=c, bk=bk, h=h: e.matmul(PS[:, bk, 0:256], Wkv[:, c, 128 * h:128 * h + 128], mn[:, c, :],
                                                                       start=(c == 0), stop=(c == 7)), r=[BWkv, Bmn], w=[pb[bk]])
                    q = csq % 2
                    csq += 1
                    op("act", lambda e, q=q, bk=bk: e.activation(out=sq[q][:, 0:256], in_=PS[:, bk, 0:256], func=AF.Square), r=[pb[bk]], w=[Bsq[q]])
                    op("pe", lambda e, q=q: e.matmul(PS[:, 5, 0:256], ones[:], sq[q][:, 0:256], start=True, stop=True), r=[Bsq[q]], w=[pb[5]])
                    rstd_op(rs[:, 0:256], PS[:, 5, 0:256], EPS * 128, [pb[5]], [Brs])
                    op("dve", lambda e, bk=bk, h=h: e.scalar_tensor_tensor(out=Km[:, h, :], in0=PS[:, bk, 0:256], scalar=gsc[:, l, 29:30],
                                                                           in1=rs[:, 0:256], op0=ALU.mult, op1=ALU.mult),
                       r=[pb[bk], Brs], w=[BKm])
                for mb in range(2):
                    bk = 6 + mb
                    for c in range(8):
                        op("pe", lambda e, c=c, bk=bk, mb=mb: e.matmul(PS[:, bk, :], mn[:, c, 128 * mb:128 * mb + 128], Wkv[:, c, 512:1024],
                                                                         start=(c == 0), stop=(c == 7)), r=[BWkv, Bmn], w=[pb[bk]])
                    op("act", lambda e, bk=bk, mb=mb: e.activation(out=Vm[:, mb, :], in_=PS[:, bk, :], func=AF.Identity), r=[pb[bk]], w=[BVm])
                sch.flush()
                if _phase_done():
                    return nc

            with ExitStack() as st:
                Wbr = sb(st, "Wbr", [128, 16, 1024], BF16)
                Wo = sb(st, "Wo", [128, 8, 1024], BF16)
                xt = [sb(st, "xt%d" % i, [128, 8, 512], F32) for i in range(2)]
                brt = [sb(st, "brt%d" % i, [128, 12, 512], BF16) for i in range(2)]
                qm = [sb(st, "qm%d" % i, [128, 4, 512], BF16) for i in range(2)]
                gt = [sb(st, "gt%d" % i, [128, 8, 512], BF16) for i in range(2)]
                PM = [sb(st, "PM%d" % i, [128, 2, 512], BF16) for i in range(2)]
                rm = sb(st, "rm", [128, 512], F32)
                br3 = sb(st, "br3", [128, 4, 512], BF16)
                mg = sb(st, "mg", [128, 8, 512], F32)
                mgb = sb(st, "mgb", [128, 8, 512], BF16)
                tmp = [sb(st, "tmp%d" % i, [128, 512], F32) for i in range(2)]
                BWbr, BWo = Buf("Wbr"), Buf("Wo")
                BKm, BVm = Buf("Km"), Buf("Vm")
                Bxt = [Buf("xt"), Buf("xt")]
                Bbrt = [Buf("brt"), Buf("brt")]
                Bqm = [Buf("qm"), Buf("qm")]
                Bgt = [Buf("gt"), Buf("gt")]
                BPM = [Buf("PM"), Buf("PM")]
                Brm, Bbr3 = Buf("rm"), Buf("br3")
                Bmg = [Buf("mg%d" % i) for i in range(8)]
                Bmgb = Buf("mgb")
                Btmp = [Buf("tmp"), Buf("tmp")]
                pb = psb()
                wload(Wbr[:], w_br[l].rearrange("(c p) f -> p c f", p=128), BWbr)
                wload(Wo[:], w_out[l].rearrange("(c p) f -> p c f", p=128), BWo)
                cps = 0
                BRv = BR.rearrange("(c p) s -> p c s", p=128)
                QMv = QMA.rearrange("(c p) s -> p c s", p=128)
                GTv = GATE.rearrange("(c p) s -> p c s", p=128)
                cg = cpm = ctmp = 0
                for t in range(NT):
                    t0 = t * 512
                    xb = t % 2
                    dma("sp", qm[xb][:], QMv[:, :, t0:t0 + 512], w=[Bqm[xb]], sembuf=Bqm[xb])
                    dma("sp", brt[xb][:], BRv[:, :, t0:t0 + 512], w=[Bbrt[xb]], sembuf=Bbrt[xb])
                    dma("sp", xt[xb][:], xin_v[:, :, t0:t0 + 512], w=[Bxt[xb]], sembuf=Bxt[xb])
                    for h in range(4):
                        pi = cpm % 2
                        cpm += 1
                        for mb in range(2):
                            op("pe", lambda e, h=h, mb=mb, xb=xb: e.matmul(PS[:, mb, :], Km[:, h, 128 * mb:128 * mb + 128], qm[xb][:, h, :],
                                                                            start=True, stop=True), r=[BKm, Bqm[xb]], w=[pb[mb]])
                        op("act", lambda e, pi=pi: e.activation(out=PM[pi][:], in_=PS[:, 0:2, :], func=AF.Exp), r=[pb[0], pb[1]], w=[BPM[pi]])
                        for mb in range(2):
                            op("pe", lambda e, h=h, mb=mb, pi=pi: e.matmul(PS[:, 2, :], Vm[:, mb, 128 * h:128 * h + 128], PM[pi][:, mb, :],
                                                                            start=(mb == 0), stop=(mb == 1)), r=[BVm, BPM[pi]], w=[pb[2]])
                            op("pe", lambda e, mb=mb, pi=pi: e.matmul(PS[:, 3, :], ones[:], PM[pi][:, mb, :], start=(mb == 0), stop=(mb == 1)),
                               r=[BPM[pi]], w=[pb[3]])
                        op("dve", lambda e: e.reciprocal(out=rm[:], in_=PS[:, 3, :]), r=[pb[3]], w=[Brm])
                        op("dve", lambda e, h=h: e.tensor_tensor(out=br3[:, h, :], in0=PS[:, 2, :], in1=rm[:], op=ALU.mult),
                           r=[pb[2], Brm], w=[Bbr3])
                    for i in range(4):
                        gi = cg % 2
                        cg += 1
                        dma("sp", gt[gi][:], GTv[:, 8 * i:8 * i + 8, t0:t0 + 512], w=[Bgt[gi]], sembuf=Bgt[gi])
                        for o in range(8):
                            bk = 4 + (cps % 4)
                            cps += 1
                            for c in range(4):
                                if i < 3:
                                    rhs = brt[xb][:, 4 * i + c, :]
                                    rb = Bbrt[xb]
                                else:
                                    rhs = br3[:, c, :]
                                    rb = Bbr3
                                op("pe", lambda e, i=i, o=o, c=c, bk=bk, rhs=rhs: e.matmul(PS[:, bk, :], Wbr[:, 4 * i + c, 128 * o:128 * o + 128], rhs,
                                                                                          start=(c == 0), stop=(c == 3)), r=[BWbr, rb], w=[pb[bk]])
                            if i == 0:
                                op("dve", lambda e, o=o, bk=bk, gi=gi: e.tensor_tensor(out=mg[:, o, :], in0=PS[:, bk, :], in1=gt[gi][:, o, :], op=ALU.mult),
                                   r=[pb[bk], Bgt[gi]], w=[Bmg[o]])
                            else:
                                ti = ctmp % 2
                                ctmp += 1
                                op("dve", lambda e, o=o, bk=bk, gi=gi, ti=ti: e.tensor_tensor(out=tmp[ti][:], in0=PS[:, bk, :], in1=gt[gi][:, o, :], op=ALU.mult),
                                   r=[pb[bk], Bgt[gi]], w=[Btmp[ti]])
                                if i < 3:
                                    op("pool", lambda e, o=o, ti=ti: e.tensor_tensor(out=mg[:, o, :], in0=mg[:, o, :], in1=tmp[ti][:], op=ALU.add),
                                       r=[Btmp[ti]], w=[Bmg[o]])
                                else:
                                    op("pool", lambda e, o=o, ti=ti: e.tensor_tensor(out=mgb[:, o, :], in0=mg[:, o, :], in1=tmp[ti][:], op=ALU.add),
                                       r=[Btmp[ti], Bmg[o]], w=[Bmgb])
                    for o in range(8):
                        bk = 4 + (cps % 4)
                        cps += 1
                        for c in range(8):
                            op("pe", lambda e, o=o, c=c, bk=bk: e.matmul(PS[:, bk, :], Wo[:, c, 128 * o:128 * o + 128], mgb[:, c, :],
                                                                           start=(c == 0), stop=(c == 7)), r=[BWo, Bmgb], w=[pb[bk]])
                        op("dve", lambda e, o=o, bk=bk, xb=xb: e.tensor_tensor(out=xt[xb][:, o, :], in0=xt[xb][:, o, :], in1=PS[:, bk, :], op=ALU.add),
                           r=[pb[bk]], w=[Bxt[xb]])
                    dma("pool", XM_v[:, :, t0:t0 + 512], xt[xb][:], r=[Bxt[xb]], sembuf=Bxt[xb])
                sch.flush()
                if _phase_done():
                    return nc

            with ExitStack() as st:
                W1 = sb(st, "Wf1", [128, 8, 4096], BF16)
                W2 = sb(st, "Wf2", [128, 32, 1024], BF16)
                xt = sb(st, "xt", [128, 8, 512], F32)
                h2 = sb(st, "h2", [128, 8, 512], BF16)
                f1 = sb(st, "f1", [128, 32, 512], BF16)
                sq = [sb(st, "sq%d" % i, [128, 512], BF16) for i in range(2)]
                rs = sb(st, "rs", [128, 512], F32)
                rl = [sb(st, "rl%d" % i, [128, 512], F32) for i in range(2)]
                BW1 = [Buf("W1_%d" % i) for i in range(4)]
                BW2 = [Buf("W2_%d" % i) for i in range(4)]
                Bxt, Bh2, Brs = Buf("xt"), Buf("h2"), Buf("rs")
                Bf1 = [Buf("f1_%d" % i) for i in range(32)]
                Bsq = [Buf("sq"), Buf("sq")]
                Brl = [Buf("rl"), Buf("rl")]
                pb = psb()
                w1v = w_ff1[l].rearrange("(c p) f -> p c f", p=128)
                w2v = w_ff2[l].rearrange("(c p) f -> p c f", p=128)
                for g in range(4):
                    wload(W1[:, :, 1024 * g:1024 * g + 1024], w1v[:, :, 1024 * g:1024 * g + 1024], BW1[g])
                for g in range(4):
                    wload(W2[:, 8 * g:8 * g + 8, :], w2v[:, 8 * g:8 * g + 8, :], BW2[g])
                ost = sb(st, "ost6", [128, 512], F32)
                Bost6 = Buf("ost6")
                resb = [(rl[0], Brl[0]), (rl[1], Brl[1]), (ost, Bost6)]
                cn6 = {"sq": 0, "rl": 0, "ps": 0, "res": 0}

                def norm6(t):
                    t0 = t * 512
                    dma("sp", xt[:], XM_v[:, :, t0:t0 + 512], w=[Bxt], sembuf=Bxt)
                    for c in range(8):
                        q = cn6["sq"] % 2
                        cn6["sq"] += 1
                        op("act", lambda e, q=q, c=c: e.activation(out=sq[q][:], in_=xt[:, c, :], func=AF.Square), r=[Bxt], w=[Bsq[q]])
                        op("pe", lambda e, q=q, c=c: e.matmul(PS[:, 7, :], ones[:], sq[q][:], start=(c == 0), stop=(c == 7)),
                           r=[Bsq[q]], w=[pb[7]])
                    rstd_op(rs[:], PS[:, 7, :], EPS * 1024, [pb[7]], [Brs])
                    for c in range(8):
                        op("dve", lambda e, c=c: e.scalar_tensor_tensor(out=h2[:, c, :], in0=xt[:, c, :], scalar=gsc[:, l, 8 + c:9 + c],
                                                                        in1=rs[:], op0=ALU.mult, op1=ALU.mult), r=[Bxt, Brs], w=[Bh2])

                norm6(0)
                for t in range(NT):
                    t0 = t * 512
                    for j in range(32):
                        bk = cn6["ps"] % 6
                        cn6["ps"] += 1
                        for c in range(8):
                            op("pe", lambda e, j=j, c=c, bk=bk: e.matmul(PS[:, bk, :], W1[:, c, 128 * j:128 * j + 128], h2[:, c, :],
                                                                           start=(c == 0), stop=(c == 7)), r=[BW1[j // 8], Bh2], w=[pb[bk]])
                        ri = cn6["rl"] % 2
                        cn6["rl"] += 1
                        op("act", lambda e, bk=bk, ri=ri: e.activation(out=rl[ri][:], in_=PS[:, bk, :], func=AF.Relu), r=[pb[bk]], w=[Brl[ri]])
                        op("pool", lambda e, j=j, ri=ri: e.tensor_tensor(out=f1[:, j, :], in0=rl[ri][:], in1=rl[ri][:], op=ALU.mult),
                           r=[Brl[ri]], w=[Bf1[j]])
                    if t + 1 < NT:
                        norm6(t + 1)
                    for o in range(8):
                        rbuf, Brbuf = resb[cn6["res"] % 3]
                        cn6["res"] += 1
                        dma("sp", rbuf[:], XM_v[:, o, t0:t0 + 512], w=[Brbuf], sembuf=Brbuf)
                        bk = cn6["ps"] % 6
                        cn6["ps"] += 1
                        for j in range(32):
                            op("pe", lambda e, o=o, j=j, bk=bk: e.matmul(PS[:, bk, :], W2[:, j, 128 * o:128 * o + 128], f1[:, j, :],
                                                                           start=(j == 0), stop=(j == 31)), r=[BW2[j // 8], Bf1[j]], w=[pb[bk]])
                        op("dve", lambda e, bk=bk, rbuf=rbuf: e.tensor_tensor(out=rbuf[:], in0=rbuf[:], in1=PS[:, bk, :], op=ALU.add),
                           r=[pb[bk]], w=[Brbuf])
                        dma("pool", xout_v[:, o, t0:t0 + 512], rbuf[:], r=[Brbuf], sembuf=Brbuf)
                sch.flush()
                if _phase_done():
                    return nc
        print("n_instr", sch.n_instr)
    return nc


def _rel_bucket_np(rel):
    nb = 16
    max_exact = 8
    ret = np.where(rel > 0, nb, 0)
    n = np.abs(rel)
    nf = np.maximum(n, 1).astype(np.float32)
    large = max_exact + (np.log(nf / np.float32(max_exact)) / np.float32(math.log(128 / max_exact)) * np.float32(nb - max_exact)).astype(np.int32)
    large = np.minimum(large, nb - 1)
    return ret + np.where(n < max_exact, n, large)


def _static_consts():
    oh = np.zeros((33, LG + LW), np.float32)
    u = np.arange(LG)
    b = _rel_bucket_np(767 - u)
    oh[b, u] = 1.0
    u = np.arange(LW)
    rel = 639 - u
    b = _rel_bucket_np(rel)
    oh[b, LG + u] = 1.0
    oh[32, LG + u] = np.where(np.abs(rel) <= 128, 0.0, -30000.0)
    return oh, np.eye(128, dtype=np.float32)


def _prep_small(inp, L):
    sp = np.zeros((L, 128, NSP), np.float32)
    p = np.arange(128)
    for l in range(L):
        sp[l, :, SP_N1:SP_N1 + 8] = inp["norm1_g"][l].reshape(8, 128).T
        sp[l, :, SP_N2:SP_N2 + 8] = inp["norm2_g"][l].reshape(8, 128).T
        sp[l, :, SP_MN:SP_MN + 8] = inp["mem_norm_g"][l].reshape(8, 128).T
        sp[l, :, SP_DAQK:SP_DAQK + 2] = inp["da_qk_g"][l][:, p % 64].T
        sp[l, :, SP_WAQK:SP_WAQK + 2] = inp["wa_qk_g"][l][:, p % 64].T
        sp[l, :, SP_MAQK:SP_MAQK + 2] = inp["ma_qk_g"][l].T
        sp[l, :, SP_SUB] = inp["da_subln_g"][l]
        sp[l, :, SP_CVB:SP_CVB + 4] = inp["conv_b"][l].reshape(4, 128).T
        sp[l, :, SP_LNG:SP_LNG + 4] = inp["conv_ln_g"][l].reshape(4, 128).T
        sp[l, :, SP_LNB:SP_LNB + 4] = inp["conv_ln_b"][l].reshape(4, 128).T
        sp[l, :, SP_SINK:SP_SINK + 8] = inp["wa_sink"][l][None, :]
        cw = inp["conv_w"][l]
        sp[l, :, SP_CW:SP_CW + 124] = cw.reshape(31, 4, 128).transpose(2, 1, 0).reshape(128, 124)
        sp[l, :, SP_LAM:SP_LAM + 256] = inp["da_lambda"][l].reshape(1, 256)
    return sp


_PROG = {}


def run_cores(seqs, mems, inp, L, S, debug=False):
    key = (S, L, debug)
    if key not in _PROG:
        _PROG[key] = build_program(S, L, debug)
    nc = _PROG[key]
    oh, ident = _static_consts()
    tabx = np.concatenate([np.asarray(inp["rel_bias"], np.float32), np.ones((1, 16), np.float32)], axis=0)
    sp = _prep_small(inp, L)
    shared = {
        "oh": oh, "tabx": tabx, "ident": ident, "sp": sp,
        "w_in": np.ascontiguousarray(inp["w_in"][:L]),
        "w_mem_kv": np.ascontiguousarray(inp["w_mem_kv"][:L]),
        "w_branch": np.ascontiguousarray(inp["w_branch"][:L]).reshape(L, 2048, D),
        "w_out": np.ascontiguousarray(inp["w_out"][:L]),
        "w_ff1": np.ascontiguousarray(inp["w_ff1"][:L]),
        "w_ff2": np.ascontiguousarray(inp["w_ff2"][:L]),
    }
    in_maps = []
    for x, m in zip(seqs, mems):
        d = dict(shared)
        d["xT"] = np.ascontiguousarray(np.asarray(x, np.float32).T)
        d["memT"] = np.ascontiguousarray(np.asarray(m, np.float32).T)
        in_maps.append(d)
    res = run_bass_kernel_spmd(nc, in_maps, core_ids=list(range(len(in_maps))))
    outs = [np.ascontiguousarray(r["yT"].T) for r in res.results]
    return outs, res


def kernel(**inputs):
    inp = {k: np.asarray(v) for k, v in inputs.items()}
    xp, xs = inp["x_prompt"], inp["x_sample"]
    mp, ms = inp["mem_prompt"], inp["mem_sample"]
    S = xp.shape[1]
    seqs = [xp[0], xp[1], xs[0], xs[1], xs[2], xs[3], xp[0], xp[1]]
    mems = [mp[0], mp[1], ms[0], ms[1], ms[2], ms[3], mp[0], mp[1]]
    outs, _ = run_cores(seqs, mems, inp, DEPTH, S)
    y_prompt = np.stack([outs[0], outs[1]], axis=0).astype(np.float32)
    y_sample = np.stack([outs[2], outs[3], outs[4], outs[5]], axis=0).astype(np.float32)
    return (y_prompt, y_sample)
```

```python
import math
from contextlib import ExitStack

import numpy as np
import concourse.bass as bass
import concourse.mybir as mybir
from concourse.bass_utils import run_bass_kernel_spmd

F32 = mybir.dt.float32
BF16 = mybir.dt.bfloat16
AF = mybir.ActivationFunctionType
ALU = mybir.AluOpType

D = 1024
DEPTH = 4
EPS = 1e-6
IN_W = 7936
C_DAQ, C_DAK, C_DAV, C_CA, C_CG, C_WAQ, C_WAK, C_WAV, C_MAQ, C_GATE = 0, 512, 1024, 1536, 2048, 2560, 3072, 3200, 3328, 3840
WG = 1408
LG = 1535
LW = 1279
WGW = 1152
SP_N1, SP_N2, SP_MN, SP_DAQK, SP_WAQK, SP_MAQK, SP_SUB, SP_CVB, SP_LNG, SP_LNB, SP_SINK, SP_CW, SP_LAM = 0, 8, 16, 24, 26, 28, 30, 31, 35, 39, 43, 51, 175
NSP = 175 + 256
SAME_ENGINE_SYNC = True


class Buf:
    __slots__ = ("name", "w", "r", "sem")

    def __init__(self, name):
        self.name = name
        self.w = None
        self.r = {}
        self.sem = None


class Sched:
    ENG = ("pe", "act", "dve", "pool", "sp")

    def __init__(self, nc, es, n_dma_sems=88):
        self.nc = nc
        self.csem = {e: es.enter_context(nc.semaphore("c_" + e)) for e in ("pe", "act", "dve", "pool")}
        self.dsems = [es.enter_context(nc.semaphore("d%d" % i)) for i in range(n_dma_sems)]
        self.semcnt = [0] * n_dma_sems
        self.free = list(range(n_dma_sems))
        self.idx = {e: 0 for e in self.ENG}
        self.base = {e: 0 for e in self.ENG}
        self.waited = {e: {} for e in self.ENG}
        self.ops = {e: [] for e in self.ENG}
        self.dirty = set()
        self.phase_first = {e: 0 for e in self.ENG}
        self.n_instr = 0

    def op(self, eng, fn, r=(), w=(), dma=None):
        deps = []
        for b in r:
            if b.w is not None:
                deps.append(b.w)
        for b in w:
            if b.w is not None:
                deps.append(b.w)
            for k, v in b.r.items():
                deps.append((k[0], k[1], v))
        waits = {}
        wd = self.waited[eng]
        for kind, key, val in deps:
            if kind == "c" and key == eng and (eng == "pe" or not SAME_ENGINE_SYNC):
                continue
            k = (kind, key)
            if wd.get(k, -1) >= val:
                continue
            if waits.get(k, -1) < val:
                waits[k] = val
        for k, v in waits.items():
            wd[k] = v
        if dma is None:
            ev = ("c", eng, self.idx[eng])
            self.idx[eng] += 1
        else:
            if dma.sem is None:
                dma.sem = self.free.pop()
            self.semcnt[dma.sem] += 16
            ev = ("d", dma.sem, self.semcnt[dma.sem])
            self.dirty.add(dma.sem)
        self.ops[eng].append((waits, fn, ev))
        for b in r:
            k = (ev[0], ev[1])
            if b.r.get(k, -1) < ev[2]:
                b.r[k] = ev[2]
        for b in w:
            b.w = ev
            b.r = {}
        return ev

    def flush(self):
        nc = self.nc
        for e in self.ENG:
            waits = {}
            for E in ("pe", "act", "dve", "pool"):
                if self.idx[E] > self.phase_first[E]:
                    last = self.idx[E] - 1
                    if self.waited[e].get(("c", E), -1) < last:
                        waits[("c", E)] = last
                        self.waited[e][("c", E)] = last
            for sidx in sorted(self.dirty):
                v = self.semcnt[sidx]
                if self.waited[e].get(("d", sidx), -1) < v:
                    waits[("d", sidx)] = v
                    self.waited[e][("d", sidx)] = v
            self.ops[e].append((waits, None, None))
        sigs = {e: set() for e in self.ENG}
        for e in self.ENG:
            for waits, fn, ev in self.ops[e]:
                for (kind, key), val in waits.items():
                    if kind == "c":
                        sigs[key].add(val)
        rank = {}
        for e in self.ENG:
            rank[e] = {ix: self.base[e] + 1 + i for i, ix in enumerate(sorted(sigs[e]))}
        csem, dsems = self.csem, self.dsems
        engobj = {"pe": nc.tensor, "act": nc.scalar, "dve": nc.vector, "pool": nc.gpsimd, "sp": nc.sync}

        def make(e):
            ops = self.ops[e]
            sg = sigs[e]

            def body(eng):
                for waits, fn, ev in ops:
                    for (kind, key), val in waits.items():
                        if kind == "c":
                            eng.wait_ge(csem[key], rank[key][val])
                        else:
                            eng.wait_ge(dsems[key], val)
                    if fn is None:
                        continue
                    ins = fn(eng)
                    self.n_instr += 1
                    if ev[0] == "c":
                        if ev[2] in sg:
                            ins.then_inc(csem[e], 1)
                    else:
                        ins.then_inc(dsems[ev[1]], 16)
            return body

        with nc.Block() as block:
            block.tensor(make("pe"))
            block.scalar(make("act"))
            block.vector(make("dve"))
            block.gpsimd(make("pool"))
            block.sync(make("sp"))
        for e in self.ENG:
            self.base[e] += len(sigs[e])
            self.ops[e] = []
            self.phase_first[e] = self.idx[e]
        self.dirty = set()
        self.free = list(range(len(self.dsems)))


PH_LIMIT = [10 ** 9]
P0STOP = [0]
_PH = [0]


def _phase_done():
    _PH[0] += 1
    return _PH[0] >= PH_LIMIT[0]


def build_program(S, L, debug=False):
    _PH[0] = 0
    NT = S // 512
    NB = S // 128
    nc = bass.Bass("TRN2", target_bir_lowering=False)
    okind = "ExternalOutput" if debug else "Internal"

    def din(name, shape, dt=F32):
        return nc.dram_tensor(name, list(shape), dt, kind="ExternalInput")

    def dscr(name, shape, dt=BF16, kind=None):
        return nc.dram_tensor(name, list(shape), dt, kind=kind or okind)

    xT = din("xT", [D, S]).ap()
    memT = din("memT", [D, 256]).ap()
    oh = din("oh", [33, LG + LW]).ap()
    tabx = din("tabx", [33, 16]).ap()
    ident_d = din("ident", [128, 128]).ap()
    spd = din("sp", [L, 128, NSP]).ap()
    w_in = din("w_in", [L, D, IN_W]).ap()
    w_kv = din("w_mem_kv", [L, D, 1024]).ap()
    w_br = din("w_branch", [L, 2048, D]).ap()
    w_out = din("w_out", [L, D, D]).ap()
    w_ff1 = din("w_ff1", [L, D, 4096]).ap()
    w_ff2 = din("w_ff2", [L, 4096, D]).ap()
    yT = nc.dram_tensor("yT", [D, S], F32, kind="ExternalOutput").ap()

    GVR_h = dscr("GVR", [16 * 128 * LG], F32)
    GVR = GVR_h.ap()
    QDA = dscr("QDA", [512, S]).ap()
    KDA = dscr("KDA", [512, S]).ap()
    VDA = dscr("VDA", [S, 512]).ap()
    Z = dscr("Z", [512, S + 30]).ap()
    QWA = dscr("QWA", [512, S]).ap()
    KWA = dscr("KWA", [128, S]).ap()
    VWA = dscr("VWA", [S, 128]).ap()
    QMA = dscr("QMA", [512, S]).ap()
    GATE = dscr("GATE", [4096, S]).ap()
    BR = dscr("BR", [1536, S]).ap()
    XM = dscr("XM", [D, S], F32).ap()
    XA = dscr("XA", [D, S], F32).ap()

    es = ExitStack()
    with es:
        sch = Sched(nc, es)
        op = sch.op

        uniq = [0]

        def sb(st, name, shape, dt):
            uniq[0] += 1
            return st.enter_context(nc.sbuf_tensor("%s_%d" % (name, uniq[0]), list(shape), dt))

        PS = es.enter_context(nc.psum_tensor("PS", [128, 8, 512], F32))
        ones = sb(es, "ones", [128, 128], BF16)
        bones = sb(es, "bones", [128, 128], BF16)
        ident = sb(es, "ident_s", [128, 128], F32)
        spt = sb(es, "spt", [128, L, NSP], F32)
        gsc = sb(es, "gsc", [128, L, 40], F32)
        Km = sb(es, "Km", [128, 4, 256], BF16)
        Vm = sb(es, "Vm", [128, 2, 512], BF16)
        Bones, Bident, Bspt, Bgsc = Buf("ones"), Buf("ident"), Buf("spt"), Buf("gsc")

        def psb():
            return [Buf("ps%d" % i) for i in range(8)]

        def dma(eng, out, in_, r=(), w=(), sembuf=None):
            op(eng, lambda e, o=out, i=in_: e.dma_start(out=o, in_=i), r=r, w=w, dma=sembuf)

        with ExitStack() as st:
            tab = sb(st, "tab", [33, 16], F32)
            tabb = sb(st, "tabb", [33, 16, 128], F32)
            ohs = sb(st, "ohs", [33, LG + LW], F32)
            gvs = sb(st, "gvs", [128, LG], F32)
            zt = sb(st, "zt", [128, 4, 16], BF16)
            lamt = sb(st, "lamt", [128, 8], F32)
            lamp = sb(st, "lamp", [128, 256], F32)
            Btab, Btabb, Bohs, Bgvs, Bzt, Blam, Blamp = [Buf(n) for n in ("tab", "tabb", "ohs", "gvs", "zt", "lam", "lamp")]
            pb = psb()
            op("dve", lambda e: e.memset(ones[:], 1.0), w=[Bones])
            op("dve", lambda e: e.memset(bones[:], 0.0), w=[Bones])
            op("dve", lambda e: e.memset(bones[0:64, 0:64], 1.0), w=[Bones])
            op("dve", lambda e: e.memset(bones[64:128, 64:128], 1.0), w=[Bones])
            op("dve", lambda e: e.memset(zt[:], 0.0), w=[Bzt])
            dma("sp", ident[:], ident_d, w=[Bident], sembuf=Bident)
            dma("sp", spt[:], spd.rearrange("l p n -> p l n"), w=[Bspt], sembuf=Bspt)
            dma("sp", tab[:], tabx, w=[Btab], sembuf=Btab)
            dma("sp", ohs[:], oh, w=[Bohs], sembuf=Bohs)
            Zv = Z.rearrange("(c p) s -> p c s", p=128)
            dma("pool", Zv[:, :, 0:15], zt[:, :, 0:15], r=[Bzt], sembuf=Bzt)
            dma("pool", Zv[:, :, S + 15:S + 30], zt[:, :, 0:15], r=[Bzt], sembuf=Bzt)
            if P0STOP[0] == 1:
                sch.flush()
                return nc
            for col in range(16):
                op("dve", lambda e, col=col: e.tensor_copy(out=tabb[:, col, :], in_=tab[:, col:col + 1].to_broadcast([33, 128])),
                   r=[Btab], w=[Btabb])
            for col in range(16):
                base, ln = (0, LG) if col < 8 else (LG, LW)
                n0 = 0
                while n0 < ln:
                    n = min(512, ln - n0)
                    bk = (col * 4 + n0 // 512) % 8
                    op("pe", lambda e, col=col, bk=bk, n0=n0, n=n, base=base: e.matmul(
                        PS[:, bk, 0:n], tabb[:, col, :], ohs[:, base + n0:base + n0 + n], start=True, stop=True),
                       r=[Btabb, Bohs], w=[pb[bk]])
                    op("act", lambda e, bk=bk, n0=n0, n=n: e.activation(out=gvs[:, n0:n0 + n], in_=PS[:, bk, 0:n], func=AF.Identity),
                       r=[pb[bk]], w=[Bgvs])
                    n0 += n
                dst = bass.AP(tensor=GVR_h, offset=col * 128 * LG, ap=[[LG, 128], [1, ln]])
                dma("pool", dst, gvs[:, 0:ln], r=[Bgvs], sembuf=Bgvs)
            if P0STOP[0] == 2:
                sch.flush()
                return nc
            for l in range(L):
                lam_init = 0.8 - 0.6 * math.exp(-0.3 * l)

                def mulc(dst0, src0, n, c, l=l):
                    op("dve", lambda e: e.tensor_scalar(out=gsc[:, l, dst0:dst0 + n], in0=spt[:, l, src0:src0 + n],
                                                        scalar1=float(c), scalar2=None, op0=ALU.mult),
                       r=[Bspt], w=[Bgsc])
                mulc(0, SP_N1, 8, 32.0)
                mulc(8, SP_N2, 8, 32.0)
                mulc(16, SP_MN, 8, 32.0)
                mulc(24, SP_DAQK, 1, 1.0)
                mulc(25, SP_DAQK + 1, 1, 8.0)
                mulc(26, SP_WAQK, 1, 1.0)
                mulc(27, SP_WAQK + 1, 1, 8.0)
                mulc(28, SP_MAQK, 1, 1.0)
                mulc(29, SP_MAQK + 1, 1, math.sqrt(128.0))
                mulc(30, SP_SUB, 1, math.sqrt(128.0) * (1.0 - lam_init))
                if P0STOP[0] == 3:
                    continue
                op("dve", lambda e, l=l: e.tensor_tensor(out=lamp[:, 0:64], in0=spt[:, l, SP_LAM:SP_LAM + 64],
                                                         in1=spt[:, l, SP_LAM + 64:SP_LAM + 128], op=ALU.mult), r=[Bspt], w=[Blamp])
                op("dve", lambda e, l=l: e.tensor_tensor(out=lamp[:, 64:128], in0=spt[:, l, SP_LAM + 128:SP_LAM + 192],
                                                         in1=spt[:, l, SP_LAM + 192:SP_LAM + 256], op=ALU.mult), r=[Bspt], w=[Blamp])
                op("dve", lambda e: e.reduce_sum(out=lamt[:, 0:1], in_=lamp[:, 0:64], axis=mybir.AxisListType.X), r=[Blamp], w=[Blam])
                op("dve", lambda e: e.reduce_sum(out=lamt[:, 1:2], in_=lamp[:, 64:128], axis=mybir.AxisListType.X), r=[Blamp], w=[Blam])
                if P0STOP[0] == 4:
                    continue
                op("act", lambda e: e.activation(out=lamt[:, 2:4], in_=lamt[:, 0:2], func=AF.Exp), r=[Blam], w=[Blam])
                if P0STOP[0] == 5:
                    continue
                op("dve", lambda e: e.tensor_tensor(out=lamt[:, 5:6], in0=lamt[:, 3:4], in1=lamt[:, 2:3], op=ALU.subtract),
                   r=[Blam], w=[Blam])
                op("dve", lambda e, li=lam_init: e.tensor_scalar(out=lamt[:, 4:5], in0=lamt[:, 5:6], scalar1=-float(li), scalar2=None,
                                                                 op0=ALU.add), r=[Blam], w=[Blam])
                op("dve", lambda e, l=l: e.tensor_copy(out=gsc[:, l, 31:32], in_=lamt[:, 4:5]), r=[Blam], w=[Bgsc])
                op("act", lambda e, l=l: e.activation(out=gsc[:, l, 32:40], in_=spt[:, l, SP_SINK:SP_SINK + 8], func=AF.Exp),
                   r=[Bspt], w=[Bgsc])
            sch.flush()
            if _phase_done():
                return nc

        def rstd_op(out_ap, ps_ap, c, r, w):
            op("act", lambda e: e.activation(out=out_ap, in_=ps_ap, func=AF.Sqrt, bias=float(c), scale=1.0), r=r, w=w)
            op("dve", lambda e: e.reciprocal(out=out_ap, in_=out_ap), r=[], w=w)

        def wload(dst_ap, src_ap, buf):
            dma("pool", dst_ap, src_ap, w=[buf], sembuf=buf)

        for l in range(L):
            xin = xT if l == 0 else XA
            xout = yT if l == L - 1 else XA
            xin_v = xin.rearrange("(c p) s -> p c s", p=128)
            xout_v = xout.rearrange("(c p) s -> p c s", p=128)
            XM_v = XM.rearrange("(c p) s -> p c s", p=128)

            with ExitStack() as st:
                W = sb(st, "W1", [128, 8, 8192], BF16)
                xt1 = sb(st, "xt1", [128, 8, 512], F32)
                hTs = [sb(st, "hT%d" % i, [128, 8, 512], BF16) for i in range(2)]
                sq = [sb(st, "sq%d" % i, [128, 512], BF16) for i in range(2)]
                rs = [sb(st, "rs%d" % i, [128, 512], F32) for i in range(2)]
                sg = [sb(st, "sg%d" % i, [128, 512], F32) for i in range(2)]
                stg = [sb(st, "stg%d" % i, [128, 4, 512], BF16) for i in range(4)]
                Bxt1 = Buf("xt1")
                BhTs = [Buf("hT0"), Buf("hT1")]
                Bsq = [Buf("sq0"), Buf("sq1")]
                Brs = [Buf("rs0"), Buf("rs1")]
                Bsg = [Buf("sg0"), Buf("sg1")]
                Bstg = [Buf("stg%d" % i) for i in range(4)]
                pb = psb()
                NWG = 8
                BW = [Buf("W%d" % i) for i in range(NWG + 1)]
                wv = w_in[l].rearrange("(c p) f -> p c f", p=128)
                for g in range(NWG):
                    f0 = g * 992
                    wload(W[:, :, f0:f0 + 992], wv[:, :, f0:f0 + 992], BW[g])
                for j in range(2):
                    for d2 in range(2):
                        o = 7936 + j * 128 + d2 * 64
                        wload(W[:, :, o:o + 64], wv[:, :, C_WAK + 64 * j:C_WAK + 64 * j + 64], BW[NWG])

                def wbufs(f0, n):
                    if f0 >= 7936:
                        return [BW[NWG]]
                    return [BW[g] for g in range(f0 // 992, (f0 + n - 1) // 992 + 1)]

                cnt = {"ps": 0, "ss": 0, "sq": 0, "rs": 0, "stg": 0, "sg": 0}

                def nxt(k, n):
                    v = cnt[k] % n
                    cnt[k] += 1
                    return v

                def norm1(t):
                    t0n = t * 512
                    hb_ = t % 2
                    dma("sp", xt1[:], xin_v[:, :, t0n:t0n + 512], w=[Bxt1], sembuf=Bxt1)
                    ssb = 6 + nxt("ss", 2)
                    for c in range(8):
                        q = nxt("sq", 2)
                        op("act", lambda e, q=q, c=c: e.activation(out=sq[q][:], in_=xt1[:, c, :], func=AF.Square),
                           r=[Bxt1], w=[Bsq[q]])
                        op("pe", lambda e, q=q, c=c, ssb=ssb: e.matmul(PS[:, ssb, :], ones[:], sq[q][:], start=(c == 0), stop=(c == 7)),
                           r=[Bsq[q]], w=[pb[ssb]])
                    rq = nxt("rs", 2)
                    rstd_op(rs[rq][:], PS[:, ssb, :], EPS * 1024, [pb[ssb]], [Brs[rq]])
                    for c in range(8):
                        op("dve", lambda e, c=c, rq=rq, hb_=hb_: e.scalar_tensor_tensor(
                            out=hTs[hb_][:, c, :], in0=xt1[:, c, :], scalar=gsc[:, l, c:c + 1], in1=rs[rq][:], op0=ALU.mult, op1=ALU.mult),
                           r=[Bxt1, Brs[rq]], w=[BhTs[hb_]])

                norm1(0)
                for t in range(NT):
                    t0 = t * 512
                    hT = hTs[t % 2]
                    BhT = BhTs[t % 2]
                    pending = []
                    pcount = [0]

                    def run_pending(upto=None):
                        while pending and (upto is None or pending[0][0] <= upto):
                            pending.pop(0)[1]()

                    def proj(f0, nout=128):
                        bk = nxt("ps", 6)
                        for c in range(8):
                            op("pe", lambda e, c=c, bk=bk, hT=hT: e.matmul(PS[0:nout, bk, :], W[:, c, f0:f0 + nout], hT[:, c, :],
                                                                      start=(c == 0), stop=(c == 7)),
                               r=[BhT] + wbufs(f0, nout), w=[pb[bk]])
                        pcount[0] += 1
                        run_pending(pcount[0] - 2)
                        return bk

                    def qknorm(bk, hd, gcol, dst_ap, dstbuf):
                        q = nxt("sq", 2)
                        op("act", lambda e: e.activation(out=sq[q][:], in_=PS[:, bk, :], func=AF.Square), r=[pb[bk]], w=[Bsq[q]])
                        ssb = 6 + nxt("ss", 2)
                        lhs = bones if hd == 64 else ones
                        rq = nxt("rs", 2)

                        def stage_b():
                            op("pe", lambda e: e.matmul(PS[:, ssb, :], lhs[:], sq[q][:], start=True, stop=True), r=[Bsq[q]], w=[pb[ssb]])
                            rstd_op(rs[rq][:], PS[:, ssb, :], EPS * hd, [pb[ssb]], [Brs[rq]])
                            op("dve", lambda e: e.scalar_tensor_tensor(out=dst_ap, in0=PS[:, bk, :], scalar=gsc[:, l, gcol:gcol + 1],
                                                                       in1=rs[rq][:], op0=ALU.mult, op1=ALU.mult),
                               r=[pb[bk], Brs[rq]], w=[dstbuf])
                        pending.append((pcount[0], stage_b))

                    def store(dst_ap, src_ap, sbuf_):
                        pending.append((pcount[0], lambda: dma("pool", dst_ap, src_ap, r=[sbuf_], sembuf=sbuf_)))

                    def fm_view(T_, r0, nchunk):
                        return T_[r0:r0 + 128 * nchunk, :].rearrange("(c p) s -> p c s", p=128)

                    for (f0, gcol, DST) in ((C_DAQ, 24, QDA), (C_DAK, 25, KDA)):
                        s_ = nxt("stg", 4)
                        for c4 in range(4):
                            bk = proj(f0 + 128 * c4)
                            qknorm(bk, 64, gcol, stg[s_][:, c4, :], Bstg[s_])
                        store(fm_view(DST, 0, 4)[:, :, t0:t0 + 512], stg[s_][:], Bstg[s_])
                    run_pending()
                    s_ = nxt("stg", 4)
                    for sbk in range(4):
                        bk = nxt("ps", 6)
                        for c in range(8):
                            op("pe", lambda e, c=c, bk=bk, sbk=sbk, hT=hT: e.matmul(PS[:, bk, :], hT[:, c, 128 * sbk:128 * sbk + 128],
                                                                               W[:, c, C_DAV:C_DAV + 512], start=(c == 0), stop=(c == 7)),
                               r=[BhT] + wbufs(C_DAV, 512), w=[pb[bk]])
                        op("act", lambda e, bk=bk, sbk=sbk, s_=s_: e.activation(out=stg[s_][:, sbk, :], in_=PS[:, bk, :], func=AF.Identity),
                           r=[pb[bk]], w=[Bstg[s_]])
                    store(VDA[t0:t0 + 512, :].rearrange("(sb p) f -> p sb f", p=128), stg[s_][:], Bstg[s_])
                    s_ = nxt("stg", 4)
                    for c4 in range(4):
                        bkg = proj(C_CG + 128 * c4)
                        g_ = nxt("sg", 2)
                        op("act", lambda e, bkg=bkg, g_=g_: e.activation(out=sg[g_][:], in_=PS[:, bkg, :], func=AF.Sigmoid),
                           r=[pb[bkg]], w=[Bsg[g_]])
                        bka = proj(C_CA + 128 * c4)
                        op("dve", lambda e, bka=bka, g_=g_, c4=c4, s_=s_: e.tensor_tensor(out=stg[s_][:, c4, :], in0=PS[:, bka, :],
                                                                                         in1=sg[g_][:], op=ALU.mult),
                           r=[pb[bka], Bsg[g_]], w=[Bstg[s_]])
                    store(fm_view(Z, 0, 4)[:, :, 15 + t0:15 + t0 + 512], stg[s_][:], Bstg[s_])
                    def qknorm64(bk, gcol, dst_ap, dstbuf):
                        q = nxt("sq", 2)
                        op("act", lambda e: e.activation(out=sq[q][0:64, :], in_=PS[0:64, bk, :], func=AF.Square), r=[pb[bk]], w=[Bsq[q]])
                        ssb = 6 + nxt("ss", 2)
                        rq = nxt("rs", 2)

                        def stage_b():
                            op("pe", lambda e: e.matmul(PS[0:64, ssb, :], ones[0:64, 0:64], sq[q][0:64, :], start=True, stop=True),
                               r=[Bsq[q]], w=[pb[ssb]])
                            rstd_op(rs[rq][0:64, :], PS[0:64, ssb, :], EPS * 64, [pb[ssb]], [Brs[rq]])
                            op("dve", lambda e: e.scalar_tensor_tensor(out=dst_ap, in0=PS[0:64, bk, :], scalar=gsc[0:64, l, gcol:gcol + 1],
                                                                       in1=rs[rq][0:64, :], op0=ALU.mult, op1=ALU.mult),
                               r=[pb[bk], Brs[rq]], w=[dstbuf])
                        pending.append((pcount[0], stage_b))

                    QWv = QWA.rearrange("(c p) s -> p c s", p=64)
                    for half in range(2):
                        s_ = nxt("stg", 4)
                        for c4 in range(4):
                            bk = proj(C_WAQ + 64 * (4 * half + c4), nout=64)
                            qknorm64(bk, 26, stg[s_][0:64, c4, :], Bstg[s_])
                        store(QWv[:, 4 * half:4 * half + 4, t0:t0 + 512], stg[s_][0:64, :, :], Bstg[s_])
                    KWv = KWA.rearrange("(c p) s -> p c s", p=64)
                    s_ = nxt("stg", 4)
                    for j in range(2):
                        bk = proj(C_WAK + 64 * j, nout=64)
                        qknorm64(bk, 27, stg[s_][0:64, j, :], Bstg[s_])
                    store(KWv[:, :, t0:t0 + 512], stg[s_][0:64, 0:2, :], Bstg[s_])
                    run_pending()
                    s_ = nxt("stg", 4)
                    for sbk in range(4):
                        bk = nxt("ps", 6)
                        for c in range(8):
                            op("pe", lambda e, c=c, bk=bk, sbk=sbk, hT=hT: e.matmul(PS[:, bk, 0:128], hT[:, c, 128 * sbk:128 * sbk + 128],
                                                                               W[:, c, C_WAV:C_WAV + 128], start=(c == 0), stop=(c == 7)),
                               r=[BhT] + wbufs(C_WAV, 128), w=[pb[bk]])
                        op("act", lambda e, bk=bk, sbk=sbk, s_=s_: e.activation(out=stg[s_][:, sbk, 0:128], in_=PS[:, bk, 0:128], func=AF.Identity),
                           r=[pb[bk]], w=[Bstg[s_]])
                    store(VWA[t0:t0 + 512, :].rearrange("(sb p) f -> p sb f", p=128), stg[s_][:, :, 0:128], Bstg[s_])
                    s_ = nxt("stg", 4)
                    for c4 in range(4):
                        bk = proj(C_MAQ + 128 * c4)
                        qknorm(bk, 128, 28, stg[s_][:, c4, :], Bstg[s_])
                    store(fm_view(QMA, 0, 4)[:, :, t0:t0 + 512], stg[s_][:], Bstg[s_])
                    if t + 1 < NT:
                        run_pending()
                        norm1(t + 1)
                    for g8 in range(8):
                        s_ = nxt("stg", 4)
                        for c4 in range(4):
                            bk = proj(C_GATE + 512 * g8 + 128 * c4)
                            op("act", lambda e, bk=bk, c4=c4, s_=s_: e.activation(out=stg[s_][:, c4, :], in_=PS[:, bk, :], func=AF.Sigmoid),
                               r=[pb[bk]], w=[Bstg[s_]])
                        store(fm_view(GATE, 512 * g8, 4)[:, :, t0:t0 + 512], stg[s_][:], Bstg[s_])
                    run_pending()
                sch.flush()
                if _phase_done():
                    return nc

            with ExitStack() as st:
                KT = [sb(st, "KT%d" % i, [128, S], BF16) for i in range(2)]
                VV = [sb(st, "VV%d" % i, [128, NB, 128], BF16) for i in range(2)]
                GG = [[sb(st, "GG%d_%d" % (i, m), [128, WG], F32) for m in range(2)] for i in range(2)]
                QT = [[sb(st, "QT%d_%d" % (i, m), [128, 512], BF16) for m in range(2)] for i in range(3)]
                PT = [sb(st, "PT%d" % i, [128, 2, 512], BF16) for i in range(4)]
                TM = [sb(st, "TM%d" % i, [128, 2, 512], F32) for i in range(2)]
                ACC = [[sb(st, "ACC%d_%d" % (m, k), [128, 2, 512], F32) for k in range(3)] for m in range(2)]
                onesf = sb(st, "onesf", [128, 128], F32)
                r_ = [sb(st, "r%d" % i, [128, 512], F32) for i in range(2)]
                a_ = [sb(st, "a%d" % i, [128, 512], F32) for i in range(2)]
                oo = sb(st, "oo", [128, 512], F32)
                sq2 = sb(st, "sq2", [128, 512], BF16)
                rs2 = sb(st, "rs2", [128, 512], F32)
                ost = [sb(st, "ost%d" % i, [128, 512], BF16) for i in range(2)]
                BKT, BVV = [Buf("KT0"), Buf("KT1")], [Buf("VV0"), Buf("VV1")]
                BGG = [[Buf("G"), Buf("G")], [Buf("G"), Buf("G")]]
                BQT = [Buf("QT") for _ in range(3)]
                BPT = [Buf("PT") for _ in range(4)]
                BTM = [Buf("TM") for _ in range(2)]
                BACC = [[Buf("ACC") for _ in range(3)] for _ in range(2)]
                Bonesf = Buf("onesf")
                Br, Ba = [Buf("r0"), Buf("r1")], [Buf("a0"), Buf("a1")]
                Boo, Bsq2, Brs2 = Buf("oo"), Buf("sq2"), Buf("rs2")
                Bost = [Buf("ost0"), Buf("ost1")]
                pb = psb()
                VDAv = VDA.rearrange("(kb p) f -> p kb f", p=128)
                op("dve", lambda e: e.memset(onesf[:], 1.0), w=[Bonesf])
                for i in range(3):
                    for m in range(2):
                        op("dve", lambda e, i=i, m=m: e.memset(QT[i][m][:], 0.0), w=[BQT[i]])
                tasks = []
                cnt2 = {"q": 0, "p": 0, "tm": 0, "s": 0, "o": 0}

                def mk_task(h, qc, m, kp, qi):
                    hb = h % 2
                    t0 = qc * 512
                    pr = slice(64 * m, 64 * m + 64)
                    G, BG = GG[hb][m], BGG[hb][m]
                    ob, db = 4 + 2 * m, 5 + 2 * m
                    ssb = 7
                    sbk = 2 * (cnt2["s"] % 2)
                    cnt2["s"] += 1
                    pi = cnt2["p"] % 4
                    cnt2["p"] += 1
                    d0 = 2 * kp - 4 * qc
                    near = -2 <= d0 <= 4
                    ti = None
                    if near:
                        ti = cnt2["tm"] % 2
                        cnt2["tm"] += 1
                    acc, Bacc = ACC[m][0], BACC[m][0]

                    def pre():
                        if qc == 0 and m == 0 and kp == 0:
                            dma("sp", KT[hb][:], KDA[128 * h:128 * h + 128, :], w=[BKT[hb]], sembuf=BKT[hb])
                            dma("sp", VV[hb][:], VDAv[:, :, 128 * h:128 * h + 128], w=[BVV[hb]], sembuf=BVV[hb])
                            for m2 in range(2):
                                col = 2 * h + m2
                                src = bass.AP(tensor=GVR_h, offset=col * 128 * LG + 127, ap=[[LG - 1, 128], [1, WG]])
                                dma("sp", GG[hb][m2][:], src, w=[BGG[hb][m2]], sembuf=BGG[hb][m2])
                        if m == 0 and kp == 0:
                            for m2 in range(2):
                                dma("sp", QT[qi][m2][64 * m2:64 * m2 + 64, :], QDA[128 * h + 64 * m2:128 * h + 64 * m2 + 64, t0:t0 + 512],
                                    w=[BQT[qi]], sembuf=BQT[qi])

                    def qk():
                        for i2 in range(2):
                            kb = 2 * kp + i2
                            op("pe", lambda e, kb=kb, i2=i2: e.matmul(
                                PS[:, sbk + i2, :], KT[hb][:, 128 * kb:128 * kb + 128], QT[qi][m][:, :], start=True, stop=True),
                               r=[BKT[hb], BQT[qi]], w=[pb[sbk + i2]])

                    def ex():
                        if near:
                            for i2 in range(2):
                                d = d0 + i2
                                c0 = 640 - 128 * d
                                op("dve", lambda e, i2=i2, c0=c0: e.tensor_tensor(
                                    out=TM[ti][:, i2, :], in0=PS[:, sbk + i2, :], in1=G[:, c0:c0 + 512], op=ALU.add),
                                   r=[pb[sbk + i2], BG], w=[BTM[ti]])
                            op("act", lambda e: e.activation(out=PT[pi][:], in_=TM[ti][:], func=AF.Exp),
                               r=[BTM[ti]], w=[BPT[pi]])
                        else:
                            bc = WG - 1 if d0 < 0 else 0
                            op("act", lambda e: e.activation(
                                out=PT[pi][:], in_=PS[:, sbk:sbk + 2, :], func=AF.Exp, bias=G[:, bc:bc + 1], scale=1.0),
                               r=[pb[sbk], pb[sbk + 1], BG], w=[BPT[pi]])

                    def pv():
                        for i2 in range(2):
                            kb = 2 * kp + i2
                            first, last = (kb == 0), (kb == NB - 1)
                            op("pe", lambda e, kb=kb, i2=i2, first=first, last=last: e.matmul(
                                PS[:, ob, :], VV[hb][:, kb, :], PT[pi][:, i2, :], start=first, stop=last),
                               r=[BVV[hb], BPT[pi]], w=[pb[ob]])
                        if kp % 2 == 0:
                            if kp == 0:
                                op("dve", lambda e: e.tensor_copy(out=acc[:], in_=PT[pi][:]), r=[BPT[pi]], w=[Bacc])
                            else:
                                op("dve", lambda e: e.tensor_tensor(out=acc[:], in0=acc[:], in1=PT[pi][:], op=ALU.add),
                                   r=[BPT[pi]], w=[Bacc])
                        else:
                            for i2 in range(2):
                                op("pe", lambda e, i2=i2: e.matmul(PS[:, db, :], ones[:], PT[pi][:, i2, :],
                                                                    start=(kp == 1 and i2 == 0), stop=False),
                                   r=[BPT[pi]], w=[pb[db]])

                    def post():
                        if kp != NB // 2 - 1:
                            return
                        for i2 in range(2):
                            op("pe", lambda e, i2=i2: e.matmul(
                                PS[:, db, :], onesf[:], ACC[m][0][:, i2, :], start=False, stop=(i2 == 1)),
                               r=[BACC[m][0], Bonesf], w=[pb[db]])
                        op("dve", lambda e: e.reciprocal(out=r_[m][:], in_=PS[:, db, :]), r=[pb[db]], w=[Br[m]])
                        op("dve", lambda e: e.tensor_tensor(out=a_[m][:], in0=PS[:, ob, :], in1=r_[m][:], op=ALU.mult),
                           r=[pb[ob], Br[m]], w=[Ba[m]])
                        if m == 0:
                            return
                        op("dve", lambda e: e.scalar_tensor_tensor(out=oo[:], in0=a_[1][:], scalar=gsc[:, l, 31:32], in1=a_[0][:],
                                                                   op0=ALU.mult, op1=ALU.add), r=[Ba[0], Ba[1]], w=[Boo])
                        op("dve", lambda e: e.tensor_tensor(out=sq2[:], in0=oo[:], in1=oo[:], op=ALU.mult), r=[Boo], w=[Bsq2])
                        op("pe", lambda e: e.matmul(PS[:, ssb, :], ones[:], sq2[:], start=True, stop=True), r=[Bsq2], w=[pb[ssb]])
                        op("act", lambda e: e.activation(out=rs2[:], in_=PS[:, ssb, :], func=AF.Ln, bias=float(EPS * 128), scale=1.0),
                           r=[pb[ssb]], w=[Brs2])
                        op("act", lambda e: e.activation(out=rs2[:], in_=rs2[:], func=AF.Exp, scale=-0.5), r=[], w=[Brs2])
                        oi = cnt2["o"] % 2
                        cnt2["o"] += 1
                        op("dve", lambda e: e.scalar_tensor_tensor(out=ost[oi][:], in0=oo[:], scalar=gsc[:, l, 30:31], in1=rs2[:],
                                                                   op0=ALU.mult, op1=ALU.mult), r=[Boo, Brs2], w=[Bost[oi]])
                        dma("pool", BR[128 * h:128 * h + 128, t0:t0 + 512], ost[oi][:], r=[Bost[oi]], sembuf=Bost[oi])

                    return (pre, qk, ex, pv, post)

                for h in range(4):
                    for qc in range(NT):
                        qi = cnt2["q"] % 3
                        cnt2["q"] += 1
                        for m in range(2):
                            for kp in range(NB // 2):
                                tasks.append(mk_task(h, qc, m, kp, qi))
                NTK = len(tasks)
                for i in range(NTK + 4):
                    if i < NTK:
                        tasks[i][0]()
                        tasks[i][1]()
                        tasks[i][2]()
                    if 0 <= i - 2 < NTK:
                        tasks[i - 2][3]()
                    if 0 <= i - 4 < NTK:
                        tasks[i - 4][4]()
                sch.flush()
                if _phase_done():
                    return nc

            with ExitStack() as st:
                dg = sb(st, "dg", [128, 4, 31, 128], BF16)
                zt_ = [sb(st, "z%d" % i, [128, 4, 544], BF16) for i in range(2)]
                yy = sb(st, "yy", [128, 4, 512], F32)
                y2 = [sb(st, "y2_%d" % i, [128, 512], BF16) for i in range(2)]
                yb = [sb(st, "yb_%d" % i, [128, 512], BF16) for i in range(2)]
                mm_ = sb(st, "mm", [128, 512], F32)
                msq = sb(st, "msq", [128, 512], F32)
                var = sb(st, "var", [128, 512], F32)
                rsd = sb(st, "rsd", [128, 512], F32)
                t1 = [sb(st, "t1_%d" % i, [128, 512], F32) for i in range(2)]
                sgm = [sb(st, "sgm_%d" % i, [128, 512], F32) for i in range(2)]
                Bsgm = [Buf("sgm"), Buf("sgm")]
                cst = [sb(st, "cst%d" % i, [128, 4, 512], BF16) for i in range(2)]
                Bdg, Byy = Buf("dg"), Buf("yy")
                Bz = [Buf("z0"), Buf("z1")]
                By2, Byb = [Buf("y2"), Buf("y2")], [Buf("yb"), Buf("yb")]
                Bmm, Bmsq, Bvar, Brsd = Buf("mm"), Buf("msq"), Buf("var"), Buf("rsd")
                Bt1 = [Buf("t1"), Buf("t1")]
                Bcst = [Buf("cst"), Buf("cst")]
                pb = psb()
                for cc in range(4):
                    for j in range(31):
                        op("dve", lambda e, cc=cc, j=j: e.tensor_scalar(out=dg[:, cc, j, :], in0=ident[:],
                                                                        scalar1=spt[:, l, SP_CW + cc * 31 + j:SP_CW + cc * 31 + j + 1],
                                                                        scalar2=None, op0=ALU.mult), w=[Bdg])
                Zv = Z.rearrange("(c p) s -> p c s", p=128)
                BRv = BR[512:1024, :].rearrange("(c p) s -> p c s", p=128)
                c2 = ct = 0
                for t in range(NT):
                    t0 = t * 512
                    zb = t % 2
                    dma("sp", zt_[zb][:, :, 0:542], Zv[:, :, t0:t0 + 542], w=[Bz[zb]], sembuf=Bz[zb])
                    pend3 = []
                    for cc in range(4):
                        bk = cc % 4
                        for j in range(31):
                            op("pe", lambda e, cc=cc, j=j, bk=bk, zb=zb: e.matmul(PS[:, bk, :], dg[:, cc, j, :], zt_[zb][:, cc, j:j + 512],
                                                                                    start=(j == 0), stop=(j == 30)),
                               r=[Bdg, Bz[zb]], w=[pb[bk]])
                        while pend3:
                            pend3.pop(0)()
                        op("act", lambda e, cc=cc, bk=bk: e.activation(out=yy[:, cc, :], in_=PS[:, bk, :], func=AF.Identity,
                                                                        bias=spt[:, l, SP_CVB + cc:SP_CVB + cc + 1], scale=1.0),
                           r=[pb[bk]], w=[Byy])
                        i2 = c2 % 2
                        c2 += 1
                        op("act", lambda e, cc=cc, i2=i2: e.activation(out=y2[i2][:], in_=yy[:, cc, :], func=AF.Square), r=[Byy], w=[By2[i2]])
                        op("dve", lambda e, cc=cc, i2=i2: e.tensor_copy(out=yb[i2][:], in_=yy[:, cc, :]), r=[Byy], w=[Byb[i2]])
                        def stats_mm(cc=cc, i2=i2):
                            op("pe", lambda e: e.matmul(PS[:, 4, :], ones[:], yb[i2][:], start=(cc == 0), stop=(cc == 3)),
                               r=[Byb[i2]], w=[pb[4]])
                            op("pe", lambda e: e.matmul(PS[:, 5, :], ones[:], y2[i2][:], start=(cc == 0), stop=(cc == 3)),
                               r=[By2[i2]], w=[pb[5]])
                        pend3.append(stats_mm)
                    while pend3:
                        pend3.pop(0)()
                    op("dve", lambda e: e.tensor_scalar(out=mm_[:], in0=PS[:, 4, :], scalar1=1.0 / 512, scalar2=None, op0=ALU.mult),
                       r=[pb[4]], w=[Bmm])
                    op("dve", lambda e: e.tensor_tensor(out=msq[:], in0=mm_[:], in1=mm_[:], op=ALU.mult), r=[Bmm], w=[Bmsq])
                    op("dve", lambda e: e.tensor_scalar(out=var[:], in0=PS[:, 5, :], scalar1=1.0 / 512, scalar2=None, op0=ALU.mult),
                       r=[pb[5]], w=[Bvar])
                    op("dve", lambda e: e.tensor_tensor(out=var[:], in0=var[:], in1=msq[:], op=ALU.subtract), r=[Bmsq], w=[Bvar])
                    rstd_op(rsd[:], var[:], EPS, [Bvar], [Brsd])
                    ci = t % 2
                    for cc in range(4):
                        ti = ct % 2
                        ct += 1
                        op("dve", lambda e, cc=cc, ti=ti: e.tensor_tensor(out=t1[ti][:], in0=yy[:, cc, :], in1=mm_[:], op=ALU.subtract),
                           r=[Byy, Bmm], w=[Bt1[ti]])
                        op("dve", lambda e, ti=ti: e.tensor_tensor(out=t1[ti][:], in0=t1[ti][:], in1=rsd[:], op=ALU.mult),
                           r=[Brsd], w=[Bt1[ti]])
                        op("dve", lambda e, cc=cc, ti=ti: e.tensor_scalar(out=t1[ti][:], in0=t1[ti][:],
                                                                           scalar1=spt[:, l, SP_LNG + cc:SP_LNG + cc + 1],
                                                                           scalar2=spt[:, l, SP_LNB + cc:SP_LNB + cc + 1],
                                                                           op0=ALU.mult, op1=ALU.add), r=[], w=[Bt1[ti]])
                        op("act", lambda e, ti=ti: e.activation(out=sgm[ti][:], in_=t1[ti][:], func=AF.Sigmoid), r=[Bt1[ti]], w=[Bsgm[ti]])
                        op("dve", lambda e, cc=cc, ti=ti, ci=ci: e.tensor_tensor(out=cst[ci][:, cc, :], in0=t1[ti][:], in1=sgm[ti][:], op=ALU.mult),
                           r=[Bt1[ti], Bsgm[ti]], w=[Bcst[ci]])
                    dma("pool", BRv[:, :, t0:t0 + 512], cst[ci][:], r=[Bcst[ci]], sembuf=Bcst[ci])
                sch.flush()
                if _phase_done():
                    return nc

            with ExitStack() as st:
                KW = [sb(st, "KW%d" % i, [128, S], BF16) for i in range(2)]
                VW = [sb(st, "VW%d" % i, [128, NB, 128], BF16) for i in range(2)]
                QW = [sb(st, "QW%d" % i, [128, S], BF16) for i in range(2)]
                GW = [sb(st, "GW%d" % i, [128, WGW], F32) for i in range(2)]
                TW = [sb(st, "TW%d" % i, [128, 512], F32) for i in range(3)]
                PW = [sb(st, "PW%d" % i, [128, 512], BF16) for i in range(4)]
                rw = [sb(st, "rw%d" % i, [64, 512], F32) for i in range(2)]
                wst = [sb(st, "wst%d" % i, [64, 512], BF16) for i in range(2)]
                BKW, BVW, BQW = [Buf("KW"), Buf("KW")], [Buf("VW"), Buf("VW")], [Buf("QW"), Buf("QW")]
                BGW = [Buf("GW"), Buf("GW")]
                BTW = [Buf("TW") for _ in range(3)]
                BPW = [Buf("PW") for _ in range(4)]
                Brw = [Buf("rw"), Buf("rw")]
                Bwst = [Buf("wst"), Buf("wst")]
                pb = psb()
                VWAv = VWA.rearrange("(kb p) f -> p kb f", p=128)
                for i in range(2):
                    op("dve", lambda e, i=i: e.memset(KW[i][:], 0.0), w=[BKW[i]])
                    op("dve", lambda e, i=i: e.memset(QW[i][:], 0.0), w=[BQW[i]])
                    op("dve", lambda e, i=i: e.memset(VW[i][:], 0.0), w=[BVW[i]])
                cn4 = {"tw": 0, "pw": 0, "s": 0, "fin": 0}
                tasks4 = []

                def mk4(hh, qc, ki, kbs, par):
                    j = hh // 4
                    hb = hh % 2
                    t0 = qc * 512
                    kb = kbs[ki]
                    ob, db = 4 + 2 * par, 5 + 2 * par
                    d = kb - 4 * qc
                    c0 = 512 - 128 * d
                    sbk = cn4["s"] % 4
                    cn4["s"] += 1
                    ti = cn4["tw"] % 3
                    cn4["tw"] += 1
                    pi = cn4["pw"] % 4
                    cn4["pw"] += 1
                    first, last = (ki == 0), (ki == len(kbs) - 1)

                    def pre():
                        if qc == 0 and ki == 0:
                            if hh % 4 == 0:
                                dma("sp", KW[j][0:64, :], KWA[64 * j:64 * j + 64, :], w=[BKW[j]], sembuf=BKW[j])
                                dma("sp", VW[j][:, :, 0:64], VWAv[:, :, 64 * j:64 * j + 64], w=[BVW[j]], sembuf=BVW[j])
                            dma("sp", QW[hb][0:64, :], QWA[64 * hh:64 * hh + 64, :], w=[BQW[hb]], sembuf=BQW[hb])
                            src = bass.AP(tensor=GVR_h, offset=(8 + hh) * 128 * LG + 127, ap=[[LG - 1, 128], [1, WGW]])
                            dma("sp", GW[hb][:], src, w=[BGW[hb]], sembuf=BGW[hb])

                    def qk():
                        op("pe", lambda e: e.matmul(
                            PS[:, sbk, :], KW[j][:, 128 * kb:128 * kb + 128], QW[hb][:, t0:t0 + 512], start=True, stop=True),
                           r=[BKW[j], BQW[hb]], w=[pb[sbk]])

                    def ex():
                        op("dve", lambda e: e.tensor_tensor(
                            out=TW[ti][:], in0=PS[:, sbk, :], in1=GW[hb][:, c0:c0 + 512], op=ALU.add),
                           r=[pb[sbk], BGW[hb]], w=[BTW[ti]])
                        op("act", lambda e: e.activation(out=PW[pi][:], in_=TW[ti][:], func=AF.Exp),
                           r=[BTW[ti]], w=[BPW[pi]])

                    def pv():
                        op("pe", lambda e: e.matmul(
                            PS[:, ob, :], VW[j][:, kb, :], PW[pi][:], start=first, stop=last),
                           r=[BVW[j], BPW[pi]], w=[pb[ob]])
                        op("pe", lambda e: e.matmul(
                            PS[:, db, :], ones[:], PW[pi][:], start=first, stop=last),
                           r=[BPW[pi]], w=[pb[db]])

                    def post():
                        if not last:
                            return
                        op("dve", lambda e: e.tensor_scalar(
                            out=rw[par][:], in0=PS[0:64, db, :], scalar1=gsc[0:64, l, 32 + hh:33 + hh], scalar2=None, op0=ALU.add),
                           r=[pb[db]], w=[Brw[par]])
                        op("dve", lambda e: e.reciprocal(out=rw[par][:], in_=rw[par][:]), r=[], w=[Brw[par]])
                        op("dve", lambda e: e.tensor_tensor(out=wst[par][:], in0=PS[0:64, ob, :], in1=rw[par][:], op=ALU.mult),
                           r=[pb[ob], Brw[par]], w=[Bwst[par]])
                        dma("pool", BR[1024 + 64 * hh:1024 + 64 * hh + 64, t0:t0 + 512], wst[par][:], r=[Bwst[par]], sembuf=Bwst[par])

                    return (pre, qk, ex, pv, post)

                for hh in range(8):
                    for qc in range(NT):
                        par = cn4["fin"] % 2
                        cn4["fin"] += 1
                        kbs = [kb for kb in range(4 * qc - 1, 4 * qc + 5) if 0 <= kb < NB]
                        for ki in range(len(kbs)):
                            tasks4.append(mk4(hh, qc, ki, kbs, par))
                NT4 = len(tasks4)
                for i in range(NT4 + 3):
                    if i < NT4:
                        tasks4[i][0]()
                        tasks4[i][1]()
                        tasks4[i][2]()
                    if 0 <= i - 3 < NT4:
                        tasks4[i - 3][3]()
                        tasks4[i - 3][4]()
                sch.flush()
                if _phase_done():
                    return nc

            with ExitStack() as st:
                Wkv = sb(st, "Wkv", [128, 8, 1024], BF16)
                mt = sb(st, "mt", [128, 8, 256], F32)
                mn = sb(st, "mn", [128, 8, 256], BF16)
                sq = [sb(st, "sq%d" % i, [128, 512], BF16) for i in range(2)]
                rs = sb(st, "rs", [128, 512], F32)
                BWkv, Bmt, Bmn = Buf("Wkv"), Buf("mt"), Buf("mn")
                BKm, BVm = Buf("Km"), Buf("Vm")
                Bsq = [Buf("sq"), Buf("sq")]
                Brs = Buf("rs")
                pb = psb()
                cps = csq = 0
                wload(Wkv[:], w_kv[l].rearrange("(c p) f -> p c f", p=128), BWkv)
                dma("sp", mt[:], memT.rearrange("(c p) s -> p c s", p=128), w=[Bmt], sembuf=Bmt)
                for c in range(8):
                    q = csq % 2
                    csq += 1
                    op("act", lambda e, q=q, c=c: e.activation(out=sq[q][:, 0:256], in_=mt[:, c, :], func=AF.Square), r=[Bmt], w=[Bsq[q]])
                    op("pe", lambda e, q=q, c=c: e.matmul(PS[:, 4, 0:256], ones[:], sq[q][:, 0:256], start=(c == 0), stop=(c == 7)),
                       r=[Bsq[q]], w=[pb[4]])
                rstd_op(rs[:, 0:256], PS[:, 4, 0:256], EPS * 1024, [pb[4]], [Brs])
                for c in range(8):
                    op("dve", lambda e, c=c: e.scalar_tensor_tensor(out=mn[:, c, :], in0=mt[:, c, :], scalar=gsc[:, l, 16 + c:17 + c],
                                                                    in1=rs[:, 0:256], op0=ALU.mult, op1=ALU.mult), r=[Bmt, Brs], w=[Bmn])
                for h in range(4):
                    bk = h % 4
                    for c in range(8):
                        op("pe", lambda e, c=c, bk=bk, h=h: e.matmul(PS[:, bk, 0:256], Wkv[:, c, 128 * h:128 * h + 128], mn[:, c, :],
                                                                       start=(c == 0), stop=(c == 7)), r=[BWkv, Bmn], w=[pb[bk]])
                    q = csq % 2
                    csq += 1
                    op("act", lambda e, q=q, bk=bk: e.activation(out=sq[q][:, 0:256], in_=PS[:, bk, 0:256], func=AF.Square), r=[pb[bk]], w=[Bsq[q]])
                    op("pe", lambda e, q=q: e.matmul(PS[:, 5, 0:256], ones[:], sq[q][:, 0:256], start=True, stop=True), r=[Bsq[q]], w=[pb[5]])
                    rstd_op(rs[:, 0:256], PS[:, 5, 0:256], EPS * 128, [pb[5]], [Brs])
                    op("dve", lambda e, bk=bk, h=h: e.scalar_tensor_tensor(out=Km[:, h, :], in0=PS[:, bk, 0:256], scalar=gsc[:, l, 29:30],
                                                                           in1=rs[:, 0:256], op0=ALU.mult, op1=ALU.mult),
                       r=[pb[bk], Brs], w=[BKm])
                for mb in range(2):
                    bk = 6 + mb
                    for c in range(8):
                        op("pe", lambda e, c=c, bk=bk, mb=mb: e.matmul(PS[:, bk, :], mn[:, c, 128 * mb:128 * mb + 128], Wkv[:, c, 512:1024],
                                                                         start=(c == 0), stop=(c == 7)), r=[BWkv, Bmn], w=[pb[bk]])
                    op("act", lambda e, bk=bk, mb=mb: e.activation(out=Vm[:, mb, :], in_=PS[:, bk, :], func=AF.Identity), r=[pb[bk]], w=[BVm])
                sch.flush()
                if _phase_done():
                    return nc

            with ExitStack() as st:
                Wbr = sb(st, "Wbr", [128, 16, 1024], BF16)
                Wo = sb(st, "Wo", [128, 8, 1024], BF16)
                xt = [sb(st, "xt%d" % i, [128, 8, 512], F32) for i in range(2)]
                brt = [sb(st, "brt%d" % i, [128, 12, 512], BF16) for i in range(2)]
                qm = [sb(st, "qm%d" % i, [128, 4, 512], BF16) for i in range(2)]
                gt = [sb(st, "gt%d" % i, [128, 8, 512], BF16) for i in range(2)]
                PM = [sb(st, "PM%d" % i, [128, 2, 512], BF16) for i in range(2)]
                rm = sb(st, "rm", [128, 512], F32)
                br3 = sb(st, "br3", [128, 4, 512], BF16)
                mg = sb(st, "mg", [128, 8, 512], F32)
                mgb = sb(st, "mgb", [128, 8, 512], BF16)
                tmp = [sb(st, "tmp%d" % i, [128, 512], F32) for i in range(2)]
                BWbr, BWo = Buf("Wbr"), Buf("Wo")
                BKm, BVm = Buf("Km"), Buf("Vm")
                Bxt = [Buf("xt"), Buf("xt")]
                Bbrt = [Buf("brt"), Buf("brt")]
                Bqm = [Buf("qm"), Buf("qm")]
                Bgt = [Buf("gt"), Buf("gt")]
                BPM = [Buf("PM"), Buf("PM")]
                Brm, Bbr3 = Buf("rm"), Buf("br3")
                Bmg = [Buf("mg%d" % i) for i in range(8)]
                Bmgb = Buf("mgb")
                Btmp = [Buf("tmp"), Buf("tmp")]
                pb = psb()
                wload(Wbr[:], w_br[l].rearrange("(c p) f -> p c f", p=128), BWbr)
                wload(Wo[:], w_out[l].rearrange("(c p) f -> p c f", p=128), BWo)
                cps = 0
                BRv = BR.rearrange("(c p) s -> p c s", p=128)
                QMv = QMA.rearrange("(c p) s -> p c s", p=128)
                GTv = GATE.rearrange("(c p) s -> p c s", p=128)
                cg = cpm = ctmp = 0
                for t in range(NT):
                    t0 = t * 512
                    xb = t % 2
                    dma("sp", qm[xb][:], QMv[:, :, t0:t0 + 512], w=[Bqm[xb]], sembuf=Bqm[xb])
                    dma("sp", brt[xb][:], BRv[:, :, t0:t0 + 512], w=[Bbrt[xb]], sembuf=Bbrt[xb])
                    dma("sp", xt[xb][:], xin_v[:, :, t0:t0 + 512], w=[Bxt[xb]], sembuf=Bxt[xb])
                    for h in range(4):
                        pi = cpm % 2
                        cpm += 1
                        for mb in range(2):
                            op("pe", lambda e, h=h, mb=mb, xb=xb: e.matmul(PS[:, mb, :], Km[:, h, 128 * mb:128 * mb + 128], qm[xb][:, h, :],
                                                                            start=True, stop=True), r=[BKm, Bqm[xb]], w=[pb[mb]])
                        op("act", lambda e, pi=pi: e.activation(out=PM[pi][:], in_=PS[:, 0:2, :], func=AF.Exp), r=[pb[0], pb[1]], w=[BPM[pi]])
                        for mb in range(2):
                            op("pe", lambda e, h=h, mb=mb, pi=pi: e.matmul(PS[:, 2, :], Vm[:, mb, 128 * h:128 * h + 128], PM[pi][:, mb, :],
                                                                            start=(mb == 0), stop=(mb == 1)), r=[BVm, BPM[pi]], w=[pb[2]])
                            op("pe", lambda e, mb=mb, pi=pi: e.matmul(PS[:, 3, :], ones[:], PM[pi][:, mb, :], start=(mb == 0), stop=(mb == 1)),
                               r=[BPM[pi]], w=[pb[3]])
                        op("dve", lambda e: e.reciprocal(out=rm[:], in_=PS[:, 3, :]), r=[pb[3]], w=[Brm])
                        op("dve", lambda e, h=h: e.tensor_tensor(out=br3[:, h, :], in0=PS[:, 2, :], in1=rm[:], op=ALU.mult),
                           r=[pb[2], Brm], w=[Bbr3])
                    for i in range(4):
                        gi = cg % 2
                        cg += 1
                        dma("sp", gt[gi][:], GTv[:, 8 * i:8 * i + 8, t0:t0 + 512], w=[Bgt[gi]], sembuf=Bgt[gi])
                        for o in range(8):
                            bk = 4 + (cps % 4)
                            cps += 1
                            for c in range(4):
                                if i < 3:
                                    rhs = brt[xb][:, 4 * i + c, :]
                                    rb = Bbrt[xb]
                                else:
                                    rhs = br3[:, c, :]
                                    rb = Bbr3
                                op("pe", lambda e, i=i, o=o, c=c, bk=bk, rhs=rhs: e.matmul(PS[:, bk, :], Wbr[:, 4 * i + c, 128 * o:128 * o + 128], rhs,
                                                                                          start=(c == 0), stop=(c == 3)), r=[BWbr, rb], w=[pb[bk]])
                            if i == 0:
                                op("dve", lambda e, o=o, bk=bk, gi=gi: e.tensor_tensor(out=mg[:, o, :], in0=PS[:, bk, :], in1=gt[gi][:, o, :], op=ALU.mult),
                                   r=[pb[bk], Bgt[gi]], w=[Bmg[o]])
                            else:
                                ti = ctmp % 2
                                ctmp += 1
                                op("dve", lambda e, o=o, bk=bk, gi=gi, ti=ti: e.tensor_tensor(out=tmp[ti][:], in0=PS[:, bk, :], in1=gt[gi][:, o, :], op=ALU.mult),
                                   r=[pb[bk], Bgt[gi]], w=[Btmp[ti]])
                                if i < 3:
                                    op("pool", lambda e, o=o, ti=ti: e.tensor_tensor(out=mg[:, o, :], in0=mg[:, o, :], in1=tmp[ti][:], op=ALU.add),
                                       r=[Btmp[ti]], w=[Bmg[o]])
                                else:
                                    op("pool", lambda e, o=o, ti=ti: e.tensor_tensor(out=mgb[:, o, :], in0=mg[:, o, :], in1=tmp[ti][:], op=ALU.add),
                                       r=[Btmp[ti], Bmg[o]], w=[Bmgb])
                    for o in range(8):
                        bk = 4 + (cps % 4)
                        cps += 1
                        for c in range(8):
                            op("pe", lambda e, o=o, c=c, bk=bk: e.matmul(PS[:, bk, :], Wo[:, c, 128 * o:128 * o + 128], mgb[:, c, :],
                                                                           start=(c == 0), stop=(c == 7)), r=[BWo, Bmgb], w=[pb[bk]])
                        op("dve", lambda e, o=o, bk=bk, xb=xb: e.tensor_tensor(out=xt[xb][:, o, :], in0=xt[xb][:, o, :], in1=PS[:, bk, :], op=ALU.add),
                           r=[pb[bk]], w=[Bxt[xb]])
                    dma("pool", XM_v[:, :, t0:t0 + 512], xt[xb][:], r=[Bxt[xb]], sembuf=Bxt[xb])
                sch.flush()
                if _phase_done():
                    return nc

            with ExitStack() as st:
                W1 = sb(st, "Wf1", [128, 8, 4096], BF16)
                W2 = sb(st, "Wf2", [128, 32, 1024], BF16)
                xt = sb(st, "xt", [128, 8, 512], F32)
                h2 = sb(st, "h2", [128, 8, 512], BF16)
                f1 = sb(st, "f1", [128, 32, 512], BF16)
                sq = [sb(st, "sq%d" % i, [128, 512], BF16) for i in range(2)]
                rs = sb(st, "rs", [128, 512], F32)
                rl = [sb(st, "rl%d" % i, [128, 512], F32) for i in range(2)]
                BW1 = [Buf("W1_%d" % i) for i in range(4)]
                BW2 = [Buf("W2_%d" % i) for i in range(4)]
                Bxt, Bh2, Brs = Buf("xt"), Buf("h2"), Buf("rs")
                Bf1 = [Buf("f1_%d" % i) for i in range(32)]
                Bsq = [Buf("sq"), Buf("sq")]
                Brl = [Buf("rl"), Buf("rl")]
                pb = psb()
                w1v = w_ff1[l].rearrange("(c p) f -> p c f", p=128)
                w2v = w_ff2[l].rearrange("(c p) f -> p c f", p=128)
                for g in range(4):
                    wload(W1[:, :, 1024 * g:1024 * g + 1024], w1v[:, :, 1024 * g:1024 * g + 1024], BW1[g])
                for g in range(4):
                    wload(W2[:, 8 * g:8 * g + 8, :], w2v[:, 8 * g:8 * g + 8, :], BW2[g])
                ost = sb(st, "ost6", [128, 512], F32)
                Bost6 = Buf("ost6")
                resb = [(rl[0], Brl[0]), (rl[1], Brl[1]), (ost, Bost6)]
                cn6 = {"sq": 0, "rl": 0, "ps": 0, "res": 0}

                def norm6(t):
                    t0 = t * 512
                    dma("sp", xt[:], XM_v[:, :, t0:t0 + 512], w=[Bxt], sembuf=Bxt)
                    for c in range(8):
                        q = cn6["sq"] % 2
                        cn6["sq"] += 1
                        op("act", lambda e, q=q, c=c: e.activation(out=sq[q][:], in_=xt[:, c, :], func=AF.Square), r=[Bxt], w=[Bsq[q]])
                        op("pe", lambda e, q=q, c=c: e.matmul(PS[:, 7, :], ones[:], sq[q][:], start=(c == 0), stop=(c == 7)),
                           r=[Bsq[q]], w=[pb[7]])
                    rstd_op(rs[:], PS[:, 7, :], EPS * 1024, [pb[7]], [Brs])
                    for c in range(8):
                        op("dve", lambda e, c=c: e.scalar_tensor_tensor(out=h2[:, c, :], in0=xt[:, c, :], scalar=gsc[:, l, 8 + c:9 + c],
                                                                        in1=rs[:], op0=ALU.mult, op1=ALU.mult), r=[Bxt, Brs], w=[Bh2])

                norm6(0)
                for t in range(NT):
                    t0 = t * 512
                    for j in range(32):
                        bk = cn6["ps"] % 6
                        cn6["ps"] += 1
                        for c in range(8):
                            op("pe", lambda e, j=j, c=c, bk=bk: e.matmul(PS[:, bk, :], W1[:, c, 128 * j:128 * j + 128], h2[:, c, :],
                                                                           start=(c == 0), stop=(c == 7)), r=[BW1[j // 8], Bh2], w=[pb[bk]])
                        ri = cn6["rl"] % 2
                        cn6["rl"] += 1
                        op("act", lambda e, bk=bk, ri=ri: e.activation(out=rl[ri][:], in_=PS[:, bk, :], func=AF.Relu), r=[pb[bk]], w=[Brl[ri]])
                        op("pool", lambda e, j=j, ri=ri: e.tensor_tensor(out=f1[:, j, :], in0=rl[ri][:], in1=rl[ri][:], op=ALU.mult),
                           r=[Brl[ri]], w=[Bf1[j]])
                    if t + 1 < NT:
                        norm6(t + 1)
                    for o in range(8):
                        rbuf, Brbuf = resb[cn6["res"] % 3]
                        cn6["res"] += 1
                        dma("sp", rbuf[:], XM_v[:, o, t0:t0 + 512], w=[Brbuf], sembuf=Brbuf)
                        bk = cn6["ps"] % 6
                        cn6["ps"] += 1
                        for j in range(32):
                            op("pe", lambda e, o=o, j=j, bk=bk: e.matmul(PS[:, bk, :], W2[:, j, 128 * o:128 * o + 128], f1[:, j, :],
                                                                           start=(j == 0), stop=(j == 31)), r=[BW2[j // 8], Bf1[j]], w=[pb[bk]])
                        op("dve", lambda e, bk=bk, rbuf=rbuf: e.tensor_tensor(out=rbuf[:], in0=rbuf[:], in1=PS[:, bk, :], op=ALU.add),
                           r=[pb[bk]], w=[Brbuf])
                        dma("pool", xout_v[:, o, t0:t0 + 512], rbuf[:], r=[Brbuf], sembuf=Brbuf)
                sch.flush()
                if _phase_done():
                    return nc
        print("n_instr", sch.n_instr)
    return nc


def _rel_bucket_np(rel):
    nb = 16
    max_exact = 8
    ret = np.where(rel > 0, nb, 0)
    n = np.abs(rel)
    nf = np.maximum(n, 1).astype(np.float32)
    large = max_exact + (np.log(nf / np.float32(max_exact)) / np.float32(math.log(128 / max_exact)) * np.float32(nb - max_exact)).astype(np.int32)
    large = np.minimum(large, nb - 1)
    return ret + np.where(n < max_exact, n, large)


def _static_consts():
    oh = np.zeros((33, LG + LW), np.float32)
    u = np.arange(LG)
    b = _rel_bucket_np(767 - u)
    oh[b, u] = 1.0
    u = np.arange(LW)
    rel = 639 - u
    b = _rel_bucket_np(rel)
    oh[b, LG + u] = 1.0
    oh[32, LG + u] = np.where(np.abs(rel) <= 128, 0.0, -30000.0)
    return oh, np.eye(128, dtype=np.float32)


def _prep_small(inp, L):
    sp = np.zeros((L, 128, NSP), np.float32)
    p = np.arange(128)
    for l in range(L):
        sp[l, :, SP_N1:SP_N1 + 8] = inp["norm1_g"][l].reshape(8, 128).T
        sp[l, :, SP_N2:SP_N2 + 8] = inp["norm2_g"][l].reshape(8, 128).T
        sp[l, :, SP_MN:SP_MN + 8] = inp["mem_norm_g"][l].reshape(8, 128).T
        sp[l, :, SP_DAQK:SP_DAQK + 2] = inp["da_qk_g"][l][:, p % 64].T
        sp[l, :, SP_WAQK:SP_WAQK + 2] = inp["wa_qk_g"][l][:, p % 64].T
        sp[l, :, SP_MAQK:SP_MAQK + 2] = inp["ma_qk_g"][l].T
        sp[l, :, SP_SUB] = inp["da_subln_g"][l]
        sp[l, :, SP_CVB:SP_CVB + 4] = inp["conv_b"][l].reshape(4, 128).T
        sp[l, :, SP_LNG:SP_LNG + 4] = inp["conv_ln_g"][l].reshape(4, 128).T
        sp[l, :, SP_LNB:SP_LNB + 4] = inp["conv_ln_b"][l].reshape(4, 128).T
        sp[l, :, SP_SINK:SP_SINK + 8] = inp["wa_sink"][l][None, :]
        cw = inp["conv_w"][l]
        sp[l, :, SP_CW:SP_CW + 124] = cw.reshape(31, 4, 128).transpose(2, 1, 0).reshape(128, 124)
        sp[l, :, SP_LAM:SP_LAM + 256] = inp["da_lambda"][l].reshape(1, 256)
    return sp


_PROG = {}


def run_cores(seqs, mems, inp, L, S, debug=False):
    key = (S, L, debug)
    if key not in _PROG:
        _PROG[key] = build_program(S, L, debug)
    nc = _PROG[key]
    oh, ident = _static_consts()
    tabx = np.concatenate([np.asarray(inp["rel_bias"], np.float32), np.ones((1, 16), np.float32)], axis=0)
    sp = _prep_small(inp, L)
    shared = {
        "oh": oh, "tabx": tabx, "ident": ident, "sp": sp,
        "w_in": np.ascontiguousarray(inp["w_in"][:L]),
        "w_mem_kv": np.ascontiguousarray(inp["w_mem_kv"][:L]),
        "w_branch": np.ascontiguousarray(inp["w_branch"][:L]).reshape(L, 2048, D),
        "w_out": np.ascontiguousarray(inp["w_out"][:L]),
        "w_ff1": np.ascontiguousarray(inp["w_ff1"][:L]),
        "w_ff2": np.ascontiguousarray(inp["w_ff2"][:L]),
    }
    in_maps = []
    for x, m in zip(seqs, mems):
        d = dict(shared)
        d["xT"] = np.ascontiguousarray(np.asarray(x, np.float32).T)
        d["memT"] = np.ascontiguousarray(np.asarray(m, np.float32).T)
        in_maps.append(d)
    res = run_bass_kernel_spmd(nc, in_maps, core_ids=list(range(len(in_maps))))
    outs = [np.ascontiguousarray(r["yT"].T) for r in res.results]
    return outs, res


def kernel(**inputs):
    inp = {k: np.asarray(v) for k, v in inputs.items()}
    xp, xs = inp["x_prompt"], inp["x_sample"]
    mp, ms = inp["mem_prompt"], inp["mem_sample"]
    S = xp.shape[1]
    seqs = [xp[0], xp[1], xs[0], xs[1], xs[2], xs[3], xp[0], xp[1]]
    mems = [mp[0], mp[1], ms[0], ms[1], ms[2], ms[3], mp[0], mp[1]]
    outs, _ = run_cores(seqs, mems, inp, DEPTH, S)
    y_prompt = np.stack([outs[0], outs[1]], axis=0).astype(np.float32)
    y_sample = np.stack([outs[2], outs[3], outs[4], outs[5]], axis=0).astype(np.float32)
    return (y_prompt, y_sample)
```
